# Optimizing a Trainium2 kernel written in Bass

```python
import jax, jax.numpy as jnp
from jax import lax
import numpy as np

D_MODEL = 1024
BATCH = 2
SEQ = 8192
DEPTH = 2
DEC_BATCH = 16
DEC_SEQ = 16
PAST_LEN = 1024

CHUNK = 64
QBLOCK = 128
H_SB = 8
D_SB = 64
W_SB = H_SB * D_SB
H_GLA = 4
DK_GLA = 64
DV_GLA = 128
WK_GLA = H_GLA * DK_GLA
WV_GLA = H_GLA * DV_GLA
GATE_RANK = 16
GATE_TAU = 16.0
D_FF = 4 * D_MODEL
N_MOD = 6
EPS = 1e-6
IN_SIZES = (W_SB, W_SB, W_SB, WK_GLA, WK_GLA, WV_GLA, GATE_RANK, WV_GLA)
N_IN = W_SB * 3 + WK_GLA * 2 + WV_GLA * 2 + GATE_RANK

kernel_name = "hymba_stickbreaking_gla_adaln_stream"


def _rmsnorm(x, g):
    x32 = x.astype(jnp.float32)
    y = x32 * lax.rsqrt(jnp.mean(x32 * x32, axis=-1, keepdims=True) + EPS)
    return (y * g.astype(jnp.float32)).astype(x.dtype)


def _split_points():
    pts, acc = [], 0
    for s in IN_SIZES[:-1]:
        acc += s
        pts.append(acc)
    return pts


def _sb_block(qb, qpos, k, v, kpos):
    z = jnp.einsum('bqhd,bkhd->bhqk', qb, k).astype(jnp.float32) * (D_SB ** -0.5)
    mask = kpos[None, :] < qpos[:, None]
    l_neg = jnp.where(mask, jax.nn.log_sigmoid(-z), 0.0)
    between = lax.cumsum(l_neg, axis=3, reverse=True) - l_neg
    w = jnp.where(mask, jnp.exp(jax.nn.log_sigmoid(z) + between), 0.0)
    return jnp.einsum('bhqk,bkhd->bqhd', w.astype(v.dtype), v)


def _stick_breaking(q, k, v, q_offset):
    B, T, H, d = q.shape
    kpos = jnp.arange(k.shape[1])
    qb = min(QBLOCK, T)
    nb = T // qb
    qs = q.reshape(B, nb, qb, H, d).swapaxes(0, 1)
    qpos = (q_offset + jnp.arange(T)).reshape(nb, qb)
    out = lax.map(lambda a: _sb_block(a[0], a[1], k, v, kpos), (qs, qpos))
    return out.swapaxes(0, 1).reshape(B, T, H, d)


def _gla_chunk(S, inp):
    q, k, v, g = inp
    L = q.shape[1]
    b = jnp.cumsum(g, axis=1)
    causal = jnp.tril(jnp.ones((L, L), dtype=bool))[None, :, :, None, None]
    diff = b[:, :, None] - b[:, None, :]
    decay = jnp.exp(jnp.where(causal, diff, -jnp.inf))
    att = jnp.einsum('bthd,bshd,btshd->bhts', q, k, decay)
    o = jnp.einsum('bhts,bshv->bthv', att, v) + jnp.einsum('bthd,bhdv->bthv', q * jnp.exp(b), S)
    b_last = b[:, -1]
    S_new = jnp.exp(b_last)[..., None] * S + jnp.einsum(
        'bshd,bshv->bhdv', k * jnp.exp(b_last[:, None] - b), v)
    return S_new, o


def _gla(q, k, v, g, S0):
    B, T = q.shape[:2]
    L = min(CHUNK, T)
    nc = T // L

    def to_chunks(a):
        return a.reshape(B, nc, L, *a.shape[2:]).swapaxes(0, 1)

    S, o = lax.scan(_gla_chunk, S0, (to_chunks(q), to_chunks(k), to_chunks(v), to_chunks(g)))
    return o.swapaxes(0, 1).reshape(B, T, H_GLA, DV_GLA), S


def _mixer(h, p, l, k_past, v_past, S0):
    B, T, _ = h.shape
    proj = h @ p['w_in'][l]
    qa, ka, va, qg, kg, vg, gr, og = jnp.split(proj, _split_points(), axis=-1)
    qa = _rmsnorm(qa.reshape(B, T, H_SB, D_SB), p['q_norm'][l])
    ka = _rmsnorm(ka.reshape(B, T, H_SB, D_SB), p['k_norm'][l])
    va = va.reshape(B, T, H_SB, D_SB)
    if k_past is None:
        k_all, v_all, q_offset = ka, va, 0
    else:
        k_all = jnp.concatenate([k_past.astype(ka.dtype), ka], axis=1)
        v_all = jnp.concatenate([v_past.astype(va.dtype), va], axis=1)
        q_offset = k_past.shape[1]
    a_out = _stick_breaking(qa, k_all, v_all, q_offset).reshape(B, T, W_SB)
    f32 = jnp.float32
    qg = qg.reshape(B, T, H_GLA, DK_GLA).astype(f32) * (DK_GLA ** -0.5)
    kg = kg.reshape(B, T, H_GLA, DK_GLA).astype(f32)
    vg = vg.reshape(B, T, H_GLA, DV_GLA).astype(f32)
    logf = jax.nn.log_sigmoid((gr @ p['w_gate'][l] + p['b_gate'][l]).astype(f32)) / GATE_TAU
    o, S = _gla(qg, kg, vg, logf.reshape(B, T, H_GLA, DK_GLA), S0)
    o = _rmsnorm(o, p['gla_norm'][l]).astype(h.dtype) * jax.nn.silu(og.reshape(B, T, H_GLA, DV_GLA))
    merged = jnp.concatenate([a_out, o.reshape(B, T, WV_GLA)], axis=-1)
    return merged @ p['w_out'][l], ka, va, S


def _layer(x, c, p, l, k_past, v_past, S0):
    mod = jax.nn.silu(c) @ p['w_ada'][l] + p['b_ada'][l]
    sh1, sc1, g1, sh2, sc2, g2 = jnp.split(mod[:, None, :], N_MOD, axis=-1)
    h = _rmsnorm(x, p['norm_mix'][l]) * (1.0 + sc1) + sh1
    m, k_new, v_new, S = _mixer(h, p, l, k_past, v_past, S0)
    x = x + g1 * m
    h = _rmsnorm(x, p['norm_mlp'][l]) * (1.0 + sc2) + sh2
    x = x + g2 * (jnp.square(jax.nn.relu(h @ p['w_up'][l])) @ p['w_down'][l])
    return x, k_new, v_new, S


def setup_inputs(seed: int = 0) -> dict:
    key = jax.random.key(seed)
    ks = jax.random.split(key, 20)
    n = jax.random.normal
    f = jnp.float32
    return {
        'x_prompt': n(ks[0], (BATCH, SEQ, D_MODEL), f),
        'x_sample': n(ks[1], (DEC_BATCH, DEC_SEQ, D_MODEL), f),
        'c_prompt': n(ks[2], (BATCH, D_MODEL), f),
        'c_sample': n(ks[3], (DEC_BATCH, D_MODEL), f),
        'cache_k': n(ks[4], (DEPTH, DEC_BATCH, PAST_LEN, H_SB, D_SB), f),
        'cache_v': n(ks[5], (DEPTH, DEC_BATCH, PAST_LEN, H_SB, D_SB), f),
        'state_gla': 0.3 * n(ks[6], (DEPTH, DEC_BATCH, H_GLA, DK_GLA, DV_GLA), f),
        'w_ada': 0.5 * D_MODEL ** -0.5 * n(ks[7], (DEPTH, D_MODEL, N_MOD * D_MODEL), f),
        'b_ada': 0.01 * n(ks[8], (DEPTH, N_MOD * D_MODEL), f),
        'norm_mix': 1.0 + 0.05 * n(ks[9], (DEPTH, D_MODEL), f),
        'norm_mlp': 1.0 + 0.05 * n(ks[10], (DEPTH, D_MODEL), f),
        'w_in': D_MODEL ** -0.5 * n(ks[11], (DEPTH, D_MODEL, N_IN), f),
        'q_norm': 1.0 + 0.05 * n(ks[12], (DEPTH, D_SB), f),
        'k_norm': 1.0 + 0.05 * n(ks[13], (DEPTH, D_SB), f),
        'w_gate': GATE_RANK ** -0.5 * n(ks[14], (DEPTH, GATE_RANK, WK_GLA), f),
        'b_gate': 0.01 * n(ks[15], (DEPTH, WK_GLA), f),
        'gla_norm': 1.0 + 0.05 * n(ks[16], (DEPTH, DV_GLA), f),
        'w_out': D_MODEL ** -0.5 * n(ks[17], (DEPTH, W_SB + WV_GLA, D_MODEL), f),
        'w_up': D_MODEL ** -0.5 * n(ks[18], (DEPTH, D_MODEL, D_FF), f),
        'w_down': D_FF ** -0.5 * n(ks[19], (DEPTH, D_FF, D_MODEL), f),
    }


def reference(x_prompt, x_sample, c_prompt, c_sample, cache_k, cache_v, state_gla,
              w_ada, b_ada, norm_mix, norm_mlp, w_in, q_norm, k_norm, w_gate, b_gate,
              gla_norm, w_out, w_up, w_down):
    p = {'w_ada': w_ada, 'b_ada': b_ada, 'norm_mix': norm_mix, 'norm_mlp': norm_mlp,
         'w_in': w_in, 'q_norm': q_norm, 'k_norm': k_norm, 'w_gate': w_gate, 'b_gate': b_gate,
         'gla_norm': gla_norm, 'w_out': w_out, 'w_up': w_up, 'w_down': w_down}
    xp, xs = x_prompt, x_sample
    kp_l, vp_l, sp_l, ks_l, vs_l, ss_l = [], [], [], [], [], []
    for l in range(DEPTH):
        S0 = jnp.zeros((xp.shape[0], H_GLA, DK_GLA, DV_GLA), jnp.float32)
        xp, kp, vp, sp = _layer(xp, c_prompt, p, l, None, None, S0)
        xs, kn, vn, sn = _layer(xs, c_sample, p, l, cache_k[l], cache_v[l],
                                state_gla[l].astype(jnp.float32))
        kp_l.append(kp); vp_l.append(vp); sp_l.append(sp.astype(xp.dtype))
        ks_l.append(kn); vs_l.append(vn); ss_l.append(sn.astype(xs.dtype))
    k_prompt = jnp.stack(kp_l)
    v_prompt = jnp.stack(vp_l)
    gla_state_prompt = jnp.stack(sp_l)
    k_sample_new = jnp.stack(ks_l)
    v_sample_new = jnp.stack(vs_l)
    gla_state_sample = jnp.stack(ss_l)
    return (xp, xs, k_prompt, v_prompt, gla_state_prompt, k_sample_new, v_sample_new, gla_state_sample)
```

```python
import os
import numpy as np
import concourse.bass as bass
import concourse.mybir as mybir
from concourse.bass_utils import run_bass_kernel_spmd

F32 = mybir.dt.float32
BF16 = mybir.dt.bfloat16
F32R = mybir.dt.float32r
AF = mybir.ActivationFunctionType
ALU = mybir.AluOpType
AX = mybir.AxisListType

D = 1024
NIN = 3088
DFF = 4096
EPS = 1e-6
PAST = 1024
TS = 16
SAME_SYNC = os.environ.get('K_SAME', '1') == '1'
STQ = os.environ.get('K_STQ', 'act')
PIPE = os.environ.get('K_PIPE', 'both')
PEW = os.environ.get('K_PEW', 'dve')


class Res:
    def __init__(self, name, acc=False):
        self.name = name
        self.w = {}
        self.r = {}
        self.acc = acc
        self.dsem = None
        self.dcnt = 0


class Tile:
    def __init__(self, b, name, shape, dt, psum=False):
        if psum:
            self.h = b.nc.alloc_psum_tensor('t_' + name, list(shape), dt)
        else:
            self.h = b.nc.alloc_sbuf_tensor('t_' + name, list(shape), dt)
        self.res = Res(name)

    def __getitem__(self, idx):
        return self.h[idx]


class View:
    def __init__(self, ap, name):
        self.ap = ap
        self.res = Res(name)

    def __getitem__(self, idx):
        return self.ap[idx]


class B:
    def barrier(self):
        evs = {}
        for k in self.eng:
            if self.cnt[k] > 0:
                evs[('e', k)] = self.cnt[k]
        for r in self.dres:
            evs[('d', r.name)] = r.dcnt
        for k in self.eng:
            self._wait(k, evs)

    def __init__(self, nc):
        self.nc = nc
        self.eng = {'pe': nc.tensor, 'dve': nc.vector, 'act': nc.scalar, 'pool': nc.gpsimd, 'sp': nc.sync}
        self.semobj = {}
        self.cnt = {}
        self.seen = {}
        for k in self.eng:
            self.semobj[('e', k)] = nc.alloc_semaphore('sem_' + k)
            self.cnt[k] = 0
            self.seen[k] = {}
        self.dres = []

    def _deps(self, reads, writes):
        evs = {}

        def add(k, v):
            if evs.get(k, 0) < v:
                evs[k] = v
        for r in reads:
            for k, v in r.w.items():
                add(k, v)
        for w in writes:
            if not w.acc:
                for k, v in w.w.items():
                    add(k, v)
            for k, v in w.r.items():
                add(k, v)
        return evs

    def _wait(self, e, evs, force=False):
        for key, val in evs.items():
            if key == ('e', e) and (e == 'pe' or not SAME_SYNC) and not force:
                continue
            if self.seen[e].get(key, 0) >= val:
                continue
            self.eng[e].wait_ge(self.semobj[key], val)
            self.seen[e][key] = val

    def _commit(self, key, val, reads, writes):
        for r in reads:
            if r.r.get(key, 0) < val:
                r.r[key] = val
        for w in writes:
            if w.acc:
                if w.w.get(key, 0) < val:
                    w.w[key] = val
            else:
                w.w = {key: val}
                w.r = {}

    def op(self, e, fn, reads=(), writes=()):
        self._wait(e, self._deps(reads, writes))
        ins = fn(self.eng[e])
        self.cnt[e] += 1
        ins.then_inc(self.semobj[('e', e)], 1)
        self._commit(('e', e), self.cnt[e], reads, writes)

    def dma(self, q, out, in_, semres, reads=(), writes=()):
        self._wait(q, self._deps(reads, writes), force=True)
        if semres.dsem is None:
            semres.dsem = self.nc.alloc_semaphore('d_' + semres.name)
            self.semobj[('d', semres.name)] = semres.dsem
            self.dres.append(semres)
        semres.dcnt += 16
        self.eng[q].dma_start(out=out, in_=in_).then_inc(semres.dsem, 16)
        self._commit(('d', semres.name), semres.dcnt, reads, writes)

    def finish(self):
        for r in self.dres:
            self.eng['sp'].wait_ge(r.dsem, r.dcnt)


def build(NT):
    nc = bass.Bass("TRN2", target_bir_lowering=False)
    b = B(nc)
    NKS = PAST + TS

    def din(name, shape, dt=F32):
        return nc.dram_tensor(name, list(shape), dt, kind="ExternalInput").ap()

    def dout(name, shape):
        return nc.dram_tensor(name, list(shape), F32, kind="ExternalOutput").ap()

    def dscr(name, shape, dt):
        return nc.dram_tensor(name, list(shape), dt).ap()

    xp = din("xp", [NT, D])
    xs = din("xs", [2, TS, D])
    cT = din("cT", [128, 8, 3])
    ck = din("ck", [2, 2, PAST, 512])
    cv = din("cv", [2, 2, PAST, 512])
    sg = din("sg", [2, 2, 4, 64, 128])
    w_ada = din("w_ada", [2, D, 6 * D])
    b_ada_r = din("b_ada_r", [2, 3, 6 * D])
    b_adaT = din("b_adaT", [2, 128, 48])
    nmixT = din("nmixT", [2, 128, 8])
    nmlpT = din("nmlpT", [2, 128, 8])
    w_in = din("w_in", [2, D, NIN])
    qnbc_d = din("qnbc", [2, 128, 512])
    knbc_d = din("knbc", [2, 128, 512])
    w_gate = din("w_gate", [2, 16, 256])
    bgbc_d = din("bgbc", [2, 128, 256])
    gnbc_d = din("gnbc", [2, 128, 512])
    w_out = din("w_out", [2, D, D])
    w_up = din("w_up", [2, D, DFF])
    w_down = din("w_down", [2, DFF, D])
    identf_d = din("identf", [128, 128])
    cst_d = din("cst", [128, 128 + 128 + 896])
    gf_d = din("gf", [128, 128])
    sel_d = din("sel", [3, 3 * 128])

    yp = dout("yp", [NT, D])
    ys = dout("ys", [2, TS, D])
    kp_o = dout("kp", [2, NT, 512])
    vp_o = dout("vp", [2, NT, 512])
    sp_o = dout("spo", [2, 4, 64, 128])
    ks_o = dout("ks", [2, 2, TS, 512])
    vs_o = dout("vs", [2, 2, TS, 512])
    ss_o = dout("sso", [2, 2, 4, 64, 128])

    wbd = dict(w_in=dscr("w_in_bf", [2, D, NIN], BF16), w_out=dscr("w_out_bf", [2, D, D], BF16),
               w_up=dscr("w_up_bf", [2, D, DFF], BF16), w_down=dscr("w_down_bf", [2, DFF, D], BF16))
    wsrc = dict(w_in=w_in, w_out=w_out, w_up=w_up, w_down=w_down)
    rw = Res("rw", acc=True)
    seqs = []
    seqs.append(dict(T=NT, past=0, n=128, NK=NT,
                     x1=dscr("x1p", [NT, D], F32),
                     qT=dscr("qTp", [128, 4, NT], BF16), kT=dscr("kTp", [128, 4, NT], BF16),
                     v=dscr("vdp", [NT, 512], BF16), mT=dscr("mTp", [128, 8, NT], BF16)))
    for j in range(2):
        seqs.append(dict(T=TS, past=PAST, n=TS, NK=NKS,
                         x1=dscr("x1s%d" % j, [TS, D], F32),
                         qT=dscr("qTs%d" % j, [128, 4, TS], BF16), kT=dscr("kTs%d" % j, [128, 4, NKS], BF16),
                         v=dscr("vds%d" % j, [NKS, 512], BF16), mT=dscr("mTs%d" % j, [128, 8, TS], BF16)))
    for i, s in enumerate(seqs):
        s['rq'] = Res("rq%d" % i, acc=True)
        s['rk'] = Res("rk%d" % i, acc=True)
        s['rv'] = Res("rv%d" % i, acc=True)
        s['rm'] = Res("rm%d" % i, acc=True)
        s['rx'] = Res("rx%d" % i, acc=True)

    def T_(name, shape, dt=F32):
        return Tile(b, name, shape, dt)

    identf = T_("identf", [128, 128])
    identb = T_("identb", [128, 128], BF16)
    cstb = T_("cstb", [128, 1152], BF16)
    gf = T_("gf", [128, 128])
    epsT = T_("epsT", [128, 1])
    oneT = T_("oneT", [128, 1])
    onesf = T_("onesf", [128, 1])
    cTt = T_("cTt", [128, 8, 3])
    siluT = T_("siluT", [128, 8, 3])
    modT = T_("modT", [128, 48, 3])
    Amod = T_("Amod", [128, 16, 3])
    nmx = T_("nmx", [128, 16])
    badT = T_("badT", [128, 48])
    gbc = [[T_("gbc%d_%d" % (s, g), [128, D]) for g in range(2)] for s in range(3)]
    qnbc = T_("qnbc", [128, 512])
    knbc = T_("knbc", [128, 512])
    gnbc = T_("gnbc", [128, 512])
    bgbc = T_("bgbc", [128, 256])
    wg = T_("wg", [16, 256])

    wst = [T_("wst%d" % i, [128, 4, 512]) for i in range(2)]
    wbf = [T_("wbf%d" % i, [128, 8, 512], BF16) for i in range(3)]
    xt = [T_("xt%d" % i, [128, D]) for i in range(4)]
    xn = T_("xn", [128, D])
    ss1 = T_("ss1", [128, 1])
    rs1 = T_("rs1", [128, 1])
    hT = T_("hT", [128, 8, 512], BF16)
    mTt = T_("mTt", [128, 8, 512], BF16)
    arena = T_("arena", [128, 16384], BF16)
    tmpF = [T_("tmpF%d" % i, [128, 512]) for i in range(4)]
    tmpB = [T_("tmpB%d" % i, [128, 512], BF16) for i in range(4)]
    ss8 = T_("ss8", [128, 8])
    rs8 = T_("rs8", [128, 8])
    trs = [T_("trs%d" % i, [128, 4, 128], BF16) for i in range(2)]
    qkg = [T_("qkg%d" % i, [128, 512]) for i in range(4)]
    vgb = [T_("vgb%d" % i, [128, 512], BF16) for i in range(4)]
    gate = [T_("gate%d" % i, [128, 512]) for i in range(4)]
    grT = [T_("grT%d" % i, [16, 128]) for i in range(4)]
    gl1 = T_("gl1", [128, 256])
    gl2 = T_("gl2", [128, 256])
    gl3 = T_("gl3", [128, 256])
    gl4 = T_("gl4", [128, 256])
    ebl = T_("ebl", [128, 2])
    qtb = T_("qtb", [128, 256], BF16)
    ktb = T_("ktb", [128, 256], BF16)
    qkT = T_("qkT", [128, 4, 128], BF16)
    attb = T_("attb", [128, 4, 128], BF16)
    Sst = [T_("S%d" % i, [128, 2, 128]) for i in range(3)]
    Sbf = [T_("Sb%d" % i, [128, 2, 128], BF16) for i in range(3)]

    sel = View(hT[0:3, 0:2, :].rearrange("p a b -> p (a b)").bitcast(F32)[:, 0:384], "selv")
    AaccT = [T_("AaccT%d" % k, [128, 512], F32R) for k in range(4)]
    onesFT = T_("onesFT", [128, 128], F32R)
    NSTR = 2
    Es = [[View(xt[k][:, 512 * j:512 * (j + 1)], "aE%d%d" % (k, j)) for j in range(2)] for k in range(4)]
    SPs = [[View((vgb[k] if j == 0 else tmpB[k])[:, :], "aSP%d%d" % (k, j)) for j in range(2)] for k in range(4)]
    Ws = [[View(hT[:, 2 * k + j, :], "aW%d%d" % (k, j)) for j in range(2)] for k in range(4)]
    Aaccs = AaccT
    Abs = [View(mTt[:, k, :], "aAb%d" % k) for k in range(4)]
    onesF = onesFT
    qTs = [View(mTt[:, 4 + k, :], "aq%d" % k) for k in range(4)]
    nqs = [View(tmpF[k][:, :].bitcast(BF16)[:, 0:512], "anq%d" % k) for k in range(4)]
    ATs = [View(tmpF[k][:, :].bitcast(BF16)[:, 512:1024], "aAT%d" % k) for k in range(4)]

    psf = [Tile(b, "psf%d" % i, [128, 512], F32, psum=True) for i in range(6)]
    psb = [Tile(b, "psb%d" % i, [128, 1024], BF16, psum=True) for i in range(2)]
    POs = [View(psf[4][:, :], "aPO0"), View(psf[5][:, :], "aPO1")]
    PZs = [[View(psf[2 * k + j][:, :], "aPZ%d%d" % (k, j)) for j in range(2)] for k in range(2)]
    PCs = [View(psb[k][:, :].bitcast(F32), "aPC%d" % k) for k in range(2)]
    rot = dict(f=0, f4=0, b=0, t=0, tb=0, w=0, wb=0, po=0, a=0, tr=0)

    def nps():
        rot['f'] = (rot['f'] + 1) % 6
        return psf[rot['f']]

    def nps4():
        rot['f4'] = (rot['f4'] + 1) % 4
        return psf[rot['f4']]

    def npsb():
        rot['b'] = (rot['b'] + 1) % 2
        return psb[rot['b']]

    def ntmpF():
        rot['t'] = (rot['t'] + 1) % 4
        return tmpF[rot['t']]

    def ntmpB():
        rot['tb'] = (rot['tb'] + 1) % 4
        return tmpB[rot['tb']]

    def ntrs():
        rot['tr'] = (rot['tr'] + 1) % 2
        return trs[rot['tr']]

    def load(t, dst_ap, src_ap):
        b.dma(STQ, dst_ap, src_ap, t.res, writes=[t.res])

    def wload(src, cw, cast=True):
        rot['wb'] = (rot['wb'] + 1) % 3
        W = wbf[rot['wb']]
        v = src.rearrange("(kt p) c -> p kt c", p=128)
        b.dma('sp', W[:, :, :cw], v, W.res, reads=[rw], writes=[W.res])
        return W

    def precast(l):
        k = 0
        for name in ('w_in', 'w_out', 'w_up', 'w_down'):
            src = wsrc[name][l]
            dst = wbd[name][l]
            R, C = src.shape
            for r0 in range(0, R, 512):
                for c0 in range(0, C, 512):
                    cw = min(512, C - c0)
                    rot['w'] = (rot['w'] + 1) % 2
                    S = wst[rot['w']]
                    load(S, S[:, :, :cw], src[r0:r0 + 512, c0:c0 + cw].rearrange("(kt p) c -> p kt c", p=128))
                    k += 1
                    Bt = wbf[k % 3]
                    eng = 'dve' if k % 2 == 0 else 'pool'
                    b.op(eng, lambda e: e.tensor_copy(out=Bt[:, 0:4, :cw], in_=S[:, :, :cw]),
                         reads=[S.res], writes=[Bt.res])
                    b.dma(STQ, dst[r0:r0 + 512, c0:c0 + cw].rearrange("(kt p) c -> p kt c", p=128), Bt[:, 0:4, :cw],
                          Bt.res, reads=[Bt.res], writes=[rw])

    def wload_f32(src, cw):
        v = src.rearrange("(kt p) c -> p kt c", p=128)
        out = []
        for hf in range(2):
            rot['w'] = (rot['w'] + 1) % 2
            S = wst[rot['w']]
            load(S, S[:, :, :cw], v[:, hf * 4:(hf + 1) * 4, :])
            out.append(S)
        return out

    def rsqrt_to(dst, src, n, w, scale):
        b.op('act', lambda e: e.activation(out=dst[:n, :w], in_=src[:n, :w], func=AF.Sqrt,
                                           scale=scale, bias=epsT[:n, :]),
             reads=[src.res, epsT.res], writes=[dst.res])
        b.op('dve', lambda e: e.reciprocal(out=dst[:n, :w], in_=dst[:n, :w]), reads=[dst.res], writes=[dst.res])

    def transpose4(src, n, ident, dstT, col0):
        PB = npsb()
        for j in range(4):
            b.op('pe', lambda e: e.transpose(out=PB[:, j * 128:j * 128 + n], in_=src[:n, j * 128:(j + 1) * 128],
                                             identity=ident[:n, :n]),
                 reads=[src.res, ident.res], writes=[PB.res])
        b.op('act', lambda e: e.activation(
            out=dstT[:, :, col0:col0 + n],
            in_=PB[:, 0:512].rearrange("p (j t) -> p j t", t=128)[:, :, :n], func=AF.Copy),
            writes=[PB.res, dstT.res])

    def norm_to_hT(X, n, s, aoff, boff, col0):
        b.op('act', lambda e: e.activation(out=xn[:n, :], in_=X[:n, :], func=AF.Square, accum_out=ss1[:n, :]),
             reads=[X.res], writes=[xn.res, ss1.res])
        rsqrt_to(rs1, ss1, n, 1, 1.0 / D)
        b.op('dve', lambda e: e.tensor_scalar(out=xn[:n, :], in0=X[:n, :], scalar1=rs1[:n, 0:1], scalar2=None,
                                              op0=ALU.mult),
             reads=[X.res, rs1.res], writes=[xn.res])
        for half in range(2):
            P = nps()
            for j in range(4):
                f = half * 4 + j
                b.op('pe', lambda e: e.transpose(out=P[:, j * 128:j * 128 + n], in_=xn[:n, f * 128:(f + 1) * 128],
                                                 identity=identf[:n, :n]),
                     reads=[xn.res, identf.res], writes=[P.res])
            for j in range(4):
                f = half * 4 + j
                b.op('dve', lambda e: e.tensor_scalar(out=hT[:, f, col0:col0 + n], in0=P[:, j * 128:j * 128 + n],
                                                      scalar1=Amod[:, aoff + f, s:s + 1],
                                                      scalar2=modT[:, boff + f, s:s + 1],
                                                      op0=ALU.mult, op1=ALU.add),
                     reads=[Amod.res, modT.res], writes=[P.res, hT.res])

    load(identf, identf[:, :], identf_d[:, :])
    load(gf, gf[:, :], gf_d[:, :])
    load(cTt, cTt[:, :, :], cT[:, :, :])
    b.op(PEW, lambda e: e.tensor_copy(out=identb[:, :], in_=identf[:, :]), reads=[identf.res], writes=[identb.res])
    for c0, cw in ((0, 512), (512, 512), (1024, 128)):
        t = ntmpF()
        load(t, t[:, :cw], cst_d[:, c0:c0 + cw])
        b.op(PEW, lambda e: e.tensor_copy(out=cstb[:, c0:c0 + cw], in_=t[:, :cw]), reads=[t.res], writes=[cstb.res])
    Uincl = lambda k: cstb[:k, 0:k]
    onesb = lambda k: cstb[:, 128:128 + k]

    def mask(k, delta, qn):
        return cstb[:k, 256 + 384 - delta:256 + 384 - delta + qn]
    b.op('dve', lambda e: e.memset(epsT[:, :], EPS), writes=[epsT.res])
    b.op('dve', lambda e: e.memset(oneT[:, :], 1.0), writes=[oneT.res])
    b.op('dve', lambda e: e.memset(onesf[:, :], 1.0), writes=[onesf.res])
    b.op('act', lambda e: e.activation(out=siluT[:, :, :], in_=cTt[:, :, :], func=AF.Silu),
         reads=[cTt.res], writes=[siluT.res])

    def adaln(l):
        b.barrier()
        b.dma('sp', sel[:, :], sel_d[:, :], sel.res, writes=[sel.res])
        load(badT, badT[:, :], b_adaT[l])
        load(nmx, nmx[:, 0:8], nmixT[l])
        load(nmx, nmx[:, 8:16], nmlpT[l])
        load(qnbc, qnbc[:, :], qnbc_d[l])
        load(knbc, knbc[:, :], knbc_d[l])
        load(gnbc, gnbc[:, :], gnbc_d[l])
        load(bgbc, bgbc[:, :], bgbc_d[l])
        load(wg, wg[:, :], w_gate[l])
        for c in range(12):
            S2 = wload_f32(w_ada[l][:, c * 512:(c + 1) * 512], 512)
            if c in (4, 5, 10, 11):
                g = 0 if c < 6 else 1
                cc = c - 4 if c < 6 else c - 10
                badr = ntmpF()
                modr = ntmpF()
                load(badr, badr[:3, :], b_ada_r[l][:, c * 512:(c + 1) * 512])
                P = nps()
                for kt in range(8):
                    b.op('pe', lambda e: e.matmul(P[:3, :512], lhsT=siluT[:, kt, :], rhs=S2[kt // 4][:, kt % 4, :],
                                                  start=(kt == 0), stop=(kt == 7)),
                         reads=[siluT.res, S2[kt // 4].res], writes=[P.res])
                b.op('dve', lambda e: e.tensor_tensor(out=modr[:3, :], in0=P[:3, :512], in1=badr[:3, :], op=ALU.add),
                     reads=[badr.res], writes=[P.res, modr.res])
                for s in range(3):
                    P2 = nps()
                    b.op('pe', lambda e: e.matmul(P2[:, :512], lhsT=sel[:, s * 128:(s + 1) * 128], rhs=modr[:3, :],
                                                  start=True, stop=True),
                         reads=[sel.res, modr.res], writes=[P2.res])
                    b.op('act', lambda e: e.activation(out=gbc[s][g][:, cc * 512:(cc + 1) * 512], in_=P2[:, :512],
                                                       func=AF.Copy),
                         writes=[P2.res, gbc[s][g].res])
            else:
                P = nps()
                for j in range(4):
                    for kt in range(8):
                        b.op('pe', lambda e: e.matmul(P[:, j * 4:j * 4 + 3], lhsT=S2[kt // 4][:, kt % 4, j * 128:(j + 1) * 128],
                                                      rhs=siluT[:, kt, :], start=(kt == 0), stop=(kt == 7)),
                             reads=[siluT.res, S2[kt // 4].res], writes=[P.res])
                for j in range(4):
                    tl = c * 4 + j
                    b.op('dve', lambda e: e.tensor_scalar(out=modT[:, tl, :], in0=P[:, j * 4:j * 4 + 3],
                                                          scalar1=badT[:, tl:tl + 1], scalar2=None, op0=ALU.add),
                         reads=[badT.res], writes=[P.res, modT.res])
        for s in range(3):
            b.op('dve', lambda e: e.scalar_tensor_tensor(out=Amod[:, 0:8, s], in0=modT[:, 8:16, s], scalar=1.0,
                                                         in1=nmx[:, 0:8], op0=ALU.add, op1=ALU.mult),
                 reads=[modT.res, nmx.res], writes=[Amod.res])
            b.op('dve', lambda e: e.scalar_tensor_tensor(out=Amod[:, 8:16, s], in0=modT[:, 32:40, s], scalar=1.0,
                                                         in1=nmx[:, 8:16], op0=ALU.add, op1=ALU.mult),
                 reads=[modT.res, nmx.res], writes=[Amod.res])

    def xsrc(l, si, t0, n):
        if l == 0:
            return xp[t0:t0 + n, :] if si == 0 else xs[si - 1, t0:t0 + n, :]
        return seqs[si]['x1'][t0:t0 + n, :]

    def qknorm(P, n, isq):
        sq = ntmpF()
        b.op('act', lambda e: e.activation(out=sq[:n, :], in_=P[:n, :512], func=AF.Square), writes=[P.res, sq.res])
        b.op('dve', lambda e: e.tensor_reduce(out=ss8[:n, :], in_=sq[:n, :].rearrange("p (h d) -> p h d", d=64),
                                              axis=AX.X, op=ALU.add),
             reads=[sq.res], writes=[ss8.res])
        rsqrt_to(rs8, ss8, n, 8, 1.0 / 64)
        qn = ntmpF()
        b.op('dve', lambda e: e.tensor_tensor(out=qn[:n, :].rearrange("p (h d) -> p h d", d=64),
                                              in0=P[:n, :512].rearrange("p (h d) -> p h d", d=64),
                                              in1=rs8[:n, :].unsqueeze(2).to_broadcast([n, 8, 64]), op=ALU.mult),
             reads=[rs8.res], writes=[P.res, qn.res])
        return qn

    def p1_group(l, si, blocks):
        sq_ = seqs[si]
        n = sq_['n']
        past = sq_['past']
        for i, t0 in enumerate(blocks):
            b.dma('sp', xt[i][:n, :], xsrc(l, si, t0, n), xt[i].res, reads=xsrc_res(l, si), writes=[xt[i].res])
            norm_to_hT(xt[i], n, si, 0, 0, i * n)
        chunks = [(0, 512, 'qa'), (512, 512, 'ka'), (1024, 512, 'va'), (1536, 512, 'qk'), (2048, 512, 'vg'),
                  (2560, 16, 'gr'), (2576, 512, 'og')]
        for c0, cw, kind in chunks:
            W = wload(wbd['w_in'][l][:, c0:c0 + cw], cw)
            for i, t0 in enumerate(blocks):
                tok = slice(i * n, i * n + n)
                P = nps()
                if kind == 'gr':
                    for kt in range(8):
                        b.op('pe', lambda e: e.matmul(P[:16, :n], lhsT=W[:, kt, :16], rhs=hT[:, kt, tok],
                                                      start=(kt == 0), stop=(kt == 7)),
                             reads=[W.res, hT.res], writes=[P.res])
                    b.op('act', lambda e: e.activation(out=grT[i][:, :n], in_=P[:16, :n], func=AF.Copy),
                         writes=[P.res, grT[i].res])
                    continue
                for kt in range(8):
                    b.op('pe', lambda e: e.matmul(P[:n, :512], lhsT=hT[:, kt, tok], rhs=W[:, kt, :512],
                                                  start=(kt == 0), stop=(kt == 7)),
                         reads=[W.res, hT.res], writes=[P.res])
                if kind == 'qa':
                    qn = qknorm(P, n, True)
                    qb = ntmpB()
                    b.op('dve', lambda e: e.scalar_tensor_tensor(out=qb[:n, :], in0=qn[:n, :], scalar=0.125,
                                                                 in1=qnbc[:n, :], op0=ALU.mult, op1=ALU.mult),
                         reads=[qn.res, qnbc.res], writes=[qb.res])
                    tr = ntrs()
                    transpose4(qb, n, identb, tr, 0)
                    b.dma(STQ, sq_['qT'][:, :, t0:t0 + n], tr[:, :, :n], tr.res, reads=[tr.res], writes=[sq_['rq']])
                elif kind == 'ka':
                    qn = qknorm(P, n, False)
                    kf = ntmpF()
                    b.op(PEW, lambda e: e.tensor_tensor(out=kf[:n, :], in0=qn[:n, :], in1=knbc[:n, :], op=ALU.mult),
                         reads=[qn.res, knbc.res], writes=[kf.res])
                    dst = kp_o[l, t0:t0 + n, :] if si == 0 else ks_o[l, si - 1, t0:t0 + n, :]
                    b.dma(STQ, dst, kf[:n, :], kf.res, reads=[kf.res])
                    kb = ntmpB()
                    b.op(PEW, lambda e: e.tensor_copy(out=kb[:n, :], in_=kf[:n, :]), reads=[kf.res], writes=[kb.res])
                    tr = ntrs()
                    transpose4(kb, n, identb, tr, 0)
                    b.dma(STQ, sq_['kT'][:, :, past + t0:past + t0 + n], tr[:, :, :n], tr.res, reads=[tr.res],
                          writes=[sq_['rk']])
                elif kind == 'va':
                    vf = ntmpF()
                    b.op('act', lambda e: e.activation(out=vf[:n, :], in_=P[:n, :512], func=AF.Copy),
                         writes=[P.res, vf.res])
                    dst = vp_o[l, t0:t0 + n, :] if si == 0 else vs_o[l, si - 1, t0:t0 + n, :]
                    b.dma(STQ, dst, vf[:n, :], vf.res, reads=[vf.res])
                    vb = ntmpB()
                    b.op(PEW, lambda e: e.tensor_copy(out=vb[:n, :], in_=vf[:n, :]), reads=[vf.res], writes=[vb.res])
                    b.dma(STQ, sq_['v'][past + t0:past + t0 + n, :], vb[:n, :], vb.res, reads=[vb.res],
                          writes=[sq_['rv']])
                elif kind == 'qk':
                    b.op('act', lambda e: e.activation(out=qkg[i][:n, :], in_=P[:n, :512], func=AF.Copy),
                         writes=[P.res, qkg[i].res])
                elif kind == 'vg':
                    b.op('dve', lambda e: e.tensor_copy(out=vgb[i][:n, :], in_=P[:n, :512]),
                         writes=[P.res, vgb[i].res])
                elif kind == 'og':
                    gfz = ntmpF()
                    b.op('act', lambda e: e.activation(out=gfz[:n, :], in_=P[:n, :512], func=AF.Silu),
                         writes=[P.res, gfz.res])
                    b.op(PEW, lambda e: e.tensor_tensor(out=gate[i][:n, :], in0=gfz[:n, :], in1=gnbc[:n, :],
                                                           op=ALU.mult),
                         reads=[gfz.res, gnbc.res], writes=[gate[i].res])
        for i, t0 in enumerate(blocks):
            gla_block(l, si, i, t0, n)

    def gla_block(l, si, i, t0, n):
        sq_ = seqs[si]
        S = Sst[si]
        Sb = Sbf[si]
        P1 = nps()
        b.op('pe', lambda e: e.matmul(P1[:n, :256], lhsT=grT[i][:, :n], rhs=wg[:, :], start=True, stop=True),
             reads=[grT[i].res, wg.res], writes=[P1.res])
        b.op('dve', lambda e: e.tensor_tensor(out=gl1[:n, :], in0=P1[:n, :256], in1=bgbc[:n, :], op=ALU.add),
             reads=[bgbc.res], writes=[P1.res, gl1.res])
        b.op('act', lambda e: e.activation(out=gl2[:n, :], in_=gl1[:n, :], func=AF.Exp, scale=-1.0),
             reads=[gl1.res], writes=[gl2.res])
        b.op('act', lambda e: e.activation(out=gl1[:n, :], in_=gl2[:n, :], func=AF.Ln, bias=oneT[:n, :]),
             reads=[gl2.res, oneT.res], writes=[gl1.res])
        P2 = nps()
        b.op('pe', lambda e: e.matmul(P2[:n, :256], lhsT=gf[:n, :n], rhs=gl1[:n, :], start=True, stop=True),
             reads=[gf.res, gl1.res], writes=[P2.res])
        P3 = nps()
        for p in range(2):
            b.op('pe', lambda e: e.matmul(P3[:, 2 * p:2 * p + 1], lhsT=gl1[:n, p * 128:(p + 1) * 128], rhs=onesf[:n, :],
                                          start=True, stop=True),
                 reads=[gl1.res, onesf.res], writes=[P3.res])
        b.op('act', lambda e: e.activation(out=gl3[:n, :], in_=P2[:n, :256], func=AF.Exp, scale=-1.0 / 16),
             writes=[P2.res, gl3.res])
        b.op('act', lambda e: e.activation(out=gl4[:n, :], in_=P2[:n, :256], func=AF.Exp, scale=1.0 / 16),
             writes=[P2.res, gl4.res])
        b.op('act', lambda e: e.activation(out=ebl[:, :], in_=P3[:, 0:4].rearrange("p (a c) -> p a c", c=2)[:, :, 0],
                                           func=AF.Exp, scale=-1.0 / 16),
             writes=[P3.res, ebl.res])
        b.op('dve', lambda e: e.scalar_tensor_tensor(out=qtb[:n, :], in0=qkg[i][:n, 0:256], scalar=0.125,
                                                     in1=gl3[:n, :], op0=ALU.mult, op1=ALU.mult),
             reads=[qkg[i].res, gl3.res], writes=[qtb.res])
        b.op('dve', lambda e: e.tensor_tensor(out=ktb[:n, :], in0=qkg[i][:n, 256:512], in1=gl4[:n, :], op=ALU.mult),
             reads=[qkg[i].res, gl4.res], writes=[ktb.res])
        PB = npsb()
        for j, (src, c) in enumerate(((qtb, 0), (qtb, 1), (ktb, 0), (ktb, 1))):
            b.op('pe', lambda e: e.transpose(out=PB[:, j * 128:j * 128 + n], in_=src[:n, c * 128:(c + 1) * 128],
                                             identity=identb[:n, :n]),
                 reads=[src.res, identb.res], writes=[PB.res])
        b.op('act', lambda e: e.activation(out=qkT[:, :, :n],
                                           in_=PB[:, 0:512].rearrange("p (j t) -> p j t", t=128)[:, :, :n],
                                           func=AF.Copy),
             writes=[PB.res, qkT.res])
        P4 = [nps(), nps()]
        for h in range(4):
            p = h // 2
            hs = slice((h % 2) * 64, (h % 2) * 64 + 64)
            PP = P4[h % 2]
            b.op('pe', lambda e: e.matmul(PP[:n, p * 128:p * 128 + n], lhsT=qkT[hs, 2 + p, :n], rhs=qkT[hs, p, :n],
                                          start=True, stop=True),
                 reads=[qkT.res], writes=[PP.res])
        for h in range(4):
            p = h // 2
            PP = P4[h % 2]
            b.op('dve', lambda e: e.tensor_tensor(out=attb[:n, h, :n], in0=PP[:n, p * 128:p * 128 + n],
                                                  in1=gf[:n, :n], op=ALU.mult),
                 reads=[gf.res], writes=[PP.res, attb.res])
        P5 = nps()
        for h in range(4):
            p = h // 2
            hs = slice((h % 2) * 64, (h % 2) * 64 + 64)
            b.op('pe', lambda e: e.matmul(P5[:n, h * 128:(h + 1) * 128], lhsT=attb[:n, h, :n],
                                          rhs=vgb[i][:n, h * 128:(h + 1) * 128], start=True, stop=False),
                 reads=[attb.res, vgb[i].res], writes=[P5.res])
            b.op('pe', lambda e: e.matmul(P5[:n, h * 128:(h + 1) * 128], lhsT=qkT[hs, p, :n], rhs=Sb[hs, p, :],
                                          start=False, stop=True),
                 reads=[qkT.res, Sb.res], writes=[P5.res])
        P6 = nps()
        for h in range(4):
            p = h // 2
            hs = slice((h % 2) * 64, (h % 2) * 64 + 64)
            b.op('pe', lambda e: e.matmul(P6[hs, p * 128:(p + 1) * 128], lhsT=ktb[:n, h * 64:(h + 1) * 64],
                                          rhs=vgb[i][:n, h * 128:(h + 1) * 128], start=True, stop=True),
                 reads=[ktb.res, vgb[i].res], writes=[P6.res])
        sq = ntmpF()
        b.op('act', lambda e: e.activation(out=sq[:n, :], in_=P5[:n, :512], func=AF.Square), writes=[P5.res, sq.res])
        b.op('dve', lambda e: e.tensor_reduce(out=ss8[:n, 0:4], in_=sq[:n, :].rearrange("p (h d) -> p h d", d=128),
                                              axis=AX.X, op=ALU.add),
             reads=[sq.res], writes=[ss8.res])
        rsqrt_to(rs8, ss8, n, 4, 1.0 / 128)
        on = ntmpF()
        b.op('dve', lambda e: e.tensor_tensor(out=on[:n, :].rearrange("p (h d) -> p h d", d=128),
                                              in0=P5[:n, :512].rearrange("p (h d) -> p h d", d=128),
                                              in1=rs8[:n, 0:4].unsqueeze(2).to_broadcast([n, 4, 128]), op=ALU.mult),
             reads=[rs8.res], writes=[P5.res, on.res])
        mg = ntmpB()
        b.op(PEW, lambda e: e.tensor_tensor(out=mg[:n, :], in0=on[:n, :], in1=gate[i][:n, :], op=ALU.mult),
             reads=[on.res, gate[i].res], writes=[mg.res])
        tr = ntrs()
        transpose4(mg, n, identb, tr, 0)
        b.dma(STQ, sq_['mT'][:, 4:8, t0:t0 + n], tr[:, :, :n], tr.res, reads=[tr.res], writes=[sq_['rm']])
        b.op('dve', lambda e: e.tensor_tensor(out=S[:, :, :], in0=P6[:, 0:256].rearrange("p (a v) -> p a v", v=128),
                                              in1=S[:, :, :], op=ALU.add),
             reads=[S.res], writes=[P6.res, S.res])
        for p in range(2):
            b.op('dve', lambda e: e.tensor_scalar(out=S[:, p, :], in0=S[:, p, :], scalar1=ebl[:, p:p + 1],
                                                  scalar2=None, op0=ALU.mult),
                 reads=[ebl.res], writes=[S.res])
        b.op(PEW, lambda e: e.tensor_copy(out=Sb[:, :, :], in_=S[:, :, :]), reads=[S.res], writes=[Sb.res])

    def state_view(ap4):
        return ap4.rearrange("(p two) d v -> two d p v", two=2)

    def gla_init(l, si):
        S = Sst[si]
        if si == 0:
            b.op('dve', lambda e: e.memset(S[:, :, :], 0.0), writes=[S.res])
        else:
            v = state_view(sg[l, si - 1])
            for two in range(2):
                load(S, S[two * 64:(two + 1) * 64, :, :], v[two])
        b.op(PEW, lambda e: e.tensor_copy(out=Sbf[si][:, :, :], in_=S[:, :, :]), reads=[S.res], writes=[Sbf[si].res])

    def gla_out(l, si):
        S = Sst[si]
        v = state_view(sp_o[l] if si == 0 else ss_o[l, si - 1])
        for two in range(2):
            b.dma(STQ, v[two], S[two * 64:(two + 1) * 64, :, :], S.res, reads=[S.res])

    def cache_import(l, si):
        sq_ = seqs[si]
        for kb_ in range(PAST // 128):
            t = ntmpF()
            load(t, t[:, :], ck[l, si - 1, kb_ * 128:(kb_ + 1) * 128, :])
            kb = ntmpB()
            b.op(PEW, lambda e: e.tensor_copy(out=kb[:, :], in_=t[:, :]), reads=[t.res], writes=[kb.res])
            tr = ntrs()
            transpose4(kb, 128, identb, tr, 0)
            b.dma(STQ, sq_['kT'][:, :, kb_ * 128:(kb_ + 1) * 128], tr[:, :, :], tr.res, reads=[tr.res],
                  writes=[sq_['rk']])
            t2 = ntmpF()
            load(t2, t2[:, :], cv[l, si - 1, kb_ * 128:(kb_ + 1) * 128, :])
            vb = ntmpB()
            b.op(PEW, lambda e: e.tensor_copy(out=vb[:, :], in_=t2[:, :]), reads=[t2.res], writes=[vb.res])
            b.dma(STQ, sq_['v'][kb_ * 128:(kb_ + 1) * 128, :], vb[:, :], vb.res, reads=[vb.res], writes=[sq_['rv']])

    def attention(l, si):
        sq_ = seqs[si]
        NK = sq_['NK']
        past = sq_['past']
        T = sq_['T']
        QT = min(512, T)
        nfull = NK // 128
        rem = NK - nfull * 128
        kblocks = [(kb_ * 128, 128) for kb_ in range(nfull)] + ([(nfull * 128, rem)] if rem else [])
        VOFF = 8192
        b.barrier()
        b.op('dve', lambda e: e.tensor_scalar(out=onesF[:, :], in0=identf[:, :], scalar1=0.0, scalar2=1.0, op0=ALU.mult, op1=ALU.add), reads=[identf.res], writes=[onesF.res])
        for p in range(4):
            b.dma('sp', arena[:, 0:NK], sq_['kT'][:, p, :], arena.res, reads=[sq_['rk']], writes=[arena.res])
            if nfull:
                b.dma('sp', arena[:, VOFF:VOFF + nfull * 128].rearrange("j (kb c) -> j kb c", c=128),
                      sq_['v'][0:nfull * 128, p * 128:(p + 1) * 128].rearrange("(kb j) c -> j kb c", j=128),
                      arena.res, reads=[sq_['rv']], writes=[arena.res])
            if rem:
                b.dma('sp', arena[:rem, VOFF + nfull * 128:VOFF + (nfull + 1) * 128],
                      sq_['v'][nfull * 128:NK, p * 128:(p + 1) * 128], arena.res, reads=[sq_['rv']],
                      writes=[arena.res])
            nqt = T // QT
            for qt0 in range(0, nqt, NSTR):
                ctxs = []
                for k_, qt in enumerate(range(qt0, min(qt0 + NSTR, nqt))):
                    q0 = past + qt * QT
                    qTt = qTs[k_]
                    nqT = nqs[k_]
                    b.dma('sp', qTt[:, :QT], sq_['qT'][:, p, qt * QT:(qt + 1) * QT], qTt.res,
                          reads=[sq_['rq']], writes=[qTt.res])
                    b.op(PEW, lambda e: e.tensor_scalar(out=nqT[:, :QT], in0=qTt[:, :QT], scalar1=-1.0,
                                                           scalar2=None, op0=ALU.mult),
                         reads=[qTt.res], writes=[nqT.res])
                    kl = [(kbi, k0, ksz) for kbi, (k0, ksz) in enumerate(kblocks) if k0 < q0 + QT]
                    ctxs.append(dict(k=k_, qt=qt, q0=q0, qTt=qTt, nqT=nqT, PO=POs[k_], items=list(reversed(kl))))

                def stageA(c, h, idx):
                    kbi, k0, ksz = c['items'][idx]
                    k_ = c['k']
                    qTt = c['qTt']
                    hs = slice(h * 64, h * 64 + 64)
                    diag = (k0 + ksz - 1) >= c['q0']
                    delta = k0 - c['q0']
                    PZ = PZs[k_][idx % 2]
                    b.op('pe', lambda e: e.matmul(PZ[:ksz, :QT], lhsT=arena[hs, k0:k0 + ksz], rhs=qTt[hs, :QT],
                                                  start=True, stop=True),
                         reads=[arena.res, qTt.res], writes=[PZ.res])

                def stageAexp(c, h, idx):
                    kbi, k0, ksz = c['items'][idx]
                    k_ = c['k']
                    PZ = PZs[k_][idx % 2]
                    E = Es[k_][idx % 2]
                    b.op('act', lambda e: e.activation(out=E[:ksz, :QT], in_=PZ[:ksz, :QT], func=AF.Exp),
                         writes=[PZ.res, E.res])

                def stageA2(c, h, idx):
                    kbi, k0, ksz = c['items'][idx]
                    k_ = c['k']
                    diag = (k0 + ksz - 1) >= c['q0']
                    delta = k0 - c['q0']
                    E = Es[k_][idx % 2]
                    SPb = SPs[k_][idx % 2]
                    if diag:
                        b.op('act', lambda e: e.activation(out=E[:ksz, :QT], in_=E[:ksz, :QT], func=AF.Ln,
                                                           bias=oneT[:ksz, :]),
                             reads=[oneT.res], writes=[E.res])
                        b.op('dve', lambda e: e.tensor_tensor(out=SPb[:ksz, :QT], in0=E[:ksz, :QT],
                                                              in1=mask(ksz, delta, QT), op=ALU.mult),
                             reads=[E.res, cstb.res], writes=[SPb.res])
                    else:
                        b.op('act', lambda e: e.activation(out=SPb[:ksz, :QT], in_=E[:ksz, :QT], func=AF.Ln,
                                                           bias=oneT[:ksz, :]),
                             reads=[E.res, oneT.res], writes=[SPb.res])

                def stageB(c, h, idx):
                    kbi, k0, ksz = c['items'][idx]
                    k_ = c['k']
                    nqT = c['nqT']
                    PO = c['PO']
                    nit = len(c['items'])
                    hs = slice(h * 64, h * 64 + 64)
                    first = idx == 0
                    last = idx == nit - 1
                    diag = (k0 + ksz - 1) >= c['q0']
                    delta = k0 - c['q0']
                    SPb = SPs[k_][idx % 2]
                    Wt = Ws[k_][idx % 2]
                    Aacc = Aaccs[k_]
                    Ab = Abs[k_]
                    PC = PCs[k_]
                    b.op('pe', lambda e: e.matmul(PC[:ksz, :QT], lhsT=Uincl(ksz), rhs=SPb[:ksz, :QT],
                                                  start=True, stop=False),
                         reads=[cstb.res, SPb.res], writes=[PC.res])
                    if not first:
                        b.op('pe', lambda e: e.matmul(PC[:ksz, :QT], lhsT=onesb(ksz), rhs=Ab[:, :QT],
                                                      start=False, stop=False),
                             reads=[cstb.res, Ab.res], writes=[PC.res])
                    b.op('pe', lambda e: e.matmul(PC[:ksz, :QT], lhsT=arena[hs, k0:k0 + ksz], rhs=nqT[hs, :QT],
                                                  start=False, stop=True),
                         reads=[arena.res, nqT.res], writes=[PC.res])
                    c['PC'] = PC

                def stageB2(c, h, idx):
                    kbi, k0, ksz = c['items'][idx]
                    k_ = c['k']
                    PO = c['PO']
                    PC = c['PC']
                    nit = len(c['items'])
                    hs = slice(h * 64, h * 64 + 64)
                    first = idx == 0
                    last = idx == nit - 1
                    diag = (k0 + ksz - 1) >= c['q0']
                    delta = k0 - c['q0']
                    Wt = Ws[k_][idx % 2]
                    b.op('act', lambda e: e.activation(out=Wt[:ksz, :QT], in_=PC[:ksz, :QT], func=AF.Exp,
                                                       scale=-1.0),
                         writes=[PC.res, Wt.res])
                    if diag:
                        b.op('dve', lambda e: e.tensor_tensor(out=Wt[:ksz, :QT], in0=Wt[:ksz, :QT],
                                                              in1=mask(ksz, delta, QT), op=ALU.mult),
                             reads=[cstb.res], writes=[Wt.res])
                    b.op('pe', lambda e: e.matmul(PO[hs, :QT],
                                                  lhsT=arena[:ksz, VOFF + kbi * 128 + h * 64:VOFF + kbi * 128 + h * 64 + 64],
                                                  rhs=Wt[:ksz, :QT], start=first, stop=last),
                         reads=[arena.res, Wt.res], writes=[PO.res])

                def stageB3(c, h, idx, part):
                    kbi, k0, ksz = c['items'][idx]
                    k_ = c['k']
                    if idx == len(c['items']) - 1:
                        return
                    SPb = SPs[k_][idx % 2]
                    Aacc = Aaccs[k_]
                    Ab = Abs[k_]
                    if part == 0:
                        b.op(PEW, lambda e: e.tensor_tensor(out=Aacc[:ksz, :QT], in0=Aacc[:ksz, :QT].bitcast(F32),
                                                               in1=SPb[:ksz, :QT], op=ALU.add),
                             reads=[SPb.res], writes=[Aacc.res])
                    else:
                        b.op('pool', lambda e: e.tensor_copy(out=Ab[:ksz, :QT], in_=Aacc[:ksz, :QT].bitcast(F32)),
                             reads=[Aacc.res], writes=[Ab.res])

                for h in range(2):
                    for c in ctxs:
                        b.op(PEW, lambda e: e.tensor_scalar(out=Aaccs[c['k']][:, :QT], in0=gbc[0][0][:, :QT], scalar1=0.0, scalar2=None, op0=ALU.mult), reads=[gbc[0][0].res], writes=[Aaccs[c['k']].res])
                        b.op(PEW, lambda e: e.memset(Abs[c['k']][:, :QT], 0.0), writes=[Abs[c['k']].res])
                    mx = max(len(c['items']) for c in ctxs)
                    for step in range(-1, mx + 1):
                        act_b = [c for c in ctxs if 0 <= step - 1 < len(c['items'])]
                        act_a = [c for c in ctxs if 0 <= step < len(c['items'])]
                        act_z = [c for c in ctxs if step + 1 < len(c['items'])]
                        for c in act_b:
                            stageB(c, h, step - 1)
                        for c in act_z:
                            stageA(c, h, step + 1)
                        for c in act_a:
                            stageAexp(c, h, step)
                        for c in act_a:
                            stageA2(c, h, step)
                        for c in act_b:
                            stageB2(c, h, step - 1)
                        for c in act_b:
                            stageB3(c, h, step - 1, 0)
                        for c in act_b:
                            stageB3(c, h, step - 1, 1)
                for c in ctxs:
                    AT = ATs[c['k']]
                    b.op('dve', lambda e: e.tensor_copy(out=AT[:, :QT], in_=c['PO'][:, :QT]),
                         writes=[c['PO'].res, AT.res])
                    b.dma(STQ, sq_['mT'][:, p, c['qt'] * QT:(c['qt'] + 1) * QT], AT[:, :QT], AT.res, reads=[AT.res],
                          writes=[sq_['rm']])
        b.barrier()

    def p3_group(l, si, blocks):
        sq_ = seqs[si]
        n = sq_['n']
        nb = len(blocks)
        ntok = nb * n
        hid = arena
        for i, t0 in enumerate(blocks):
            b.dma('sp', xt[i][:n, :], xsrc(l, si, t0, n), xt[i].res, reads=xsrc_res(l, si), writes=[xt[i].res])
        b.dma('sp', mTt[:, :, :ntok], sq_['mT'][:, :, blocks[0]:blocks[0] + ntok], mTt.res, reads=[sq_['rm']],
              writes=[mTt.res])
        for c in range(2):
            cs = slice(c * 512, (c + 1) * 512)
            W = wload(wbd['w_out'][l][:, cs], 512)
            for i in range(nb):
                tok = slice(i * n, i * n + n)
                P = nps()
                for kt in range(8):
                    b.op('pe', lambda e: e.matmul(P[:n, :512], lhsT=mTt[:, kt, tok], rhs=W[:, kt, :],
                                                  start=(kt == 0), stop=(kt == 7)),
                         reads=[W.res, mTt.res], writes=[P.res])
                t = ntmpF()
                b.op('dve', lambda e: e.tensor_tensor(out=t[:n, :], in0=P[:n, :512], in1=gbc[si][0][:n, cs], op=ALU.mult),
                     reads=[gbc[si][0].res], writes=[P.res, t.res])
                b.op(PEW, lambda e: e.tensor_tensor(out=xt[i][:n, cs], in0=t[:n, :], in1=xt[i][:n, cs], op=ALU.add),
                     reads=[t.res], writes=[xt[i].res])
        for i in range(nb):
            norm_to_hT(xt[i], n, si, 8, 24, i * n)
        for jc in range(8):
            W = wload(wbd['w_up'][l][:, jc * 512:(jc + 1) * 512], 512)
            for jj in range(4):
                j = jc * 4 + jj
                P = nps()
                for kt in range(8):
                    b.op('pe', lambda e: e.matmul(P[:, :ntok], lhsT=W[:, kt, jj * 128:(jj + 1) * 128], rhs=hT[:, kt, :ntok],
                                                  start=(kt == 0), stop=(kt == 7)),
                         reads=[W.res, hT.res], writes=[P.res])
                t = ntmpF()
                b.op('act', lambda e: e.activation(out=t[:, :ntok], in_=P[:, :ntok], func=AF.Relu),
                     writes=[P.res, t.res])
                b.op(PEW, lambda e: e.tensor_tensor(out=hid[:, j * 512:j * 512 + ntok], in0=t[:, :ntok], in1=t[:, :ntok],
                                                       op=ALU.mult),
                     reads=[t.res], writes=[hid.res])
        for c in range(2):
            cs = slice(c * 512, (c + 1) * 512)
            PY = [nps() for _ in range(nb)]
            for jc in range(4):
                W = wload(wbd['w_down'][l][jc * 1024:(jc + 1) * 1024, cs], 512)
                for i in range(nb):
                    for jj in range(8):
                        j = jc * 8 + jj
                        b.op('pe', lambda e: e.matmul(PY[i][:n, :512], lhsT=hid[:, j * 512 + i * n:j * 512 + i * n + n],
                                                      rhs=W[:, jj, :], start=(j == 0), stop=(j == 31)),
                             reads=[W.res, hid.res], writes=[PY[i].res])
            for i in range(nb):
                t = ntmpF()
                b.op('dve', lambda e: e.tensor_tensor(out=t[:n, :], in0=PY[i][:n, :512], in1=gbc[si][1][:n, cs],
                                                      op=ALU.mult),
                     reads=[gbc[si][1].res], writes=[PY[i].res, t.res])
                b.op(PEW, lambda e: e.tensor_tensor(out=xt[i][:n, cs], in0=t[:n, :], in1=xt[i][:n, cs], op=ALU.add),
                     reads=[t.res], writes=[xt[i].res])
        for i, t0 in enumerate(blocks):
            if l == 0:
                b.dma(STQ, sq_['x1'][t0:t0 + n, :], xt[i][:n, :], xt[i].res, reads=[xt[i].res], writes=[sq_['rx']])
            else:
                dst = yp[t0:t0 + n, :] if si == 0 else ys[si - 1, t0:t0 + n, :]
                b.dma(STQ, dst, xt[i][:n, :], xt[i].res, reads=[xt[i].res])

    def xsrc_res(l, si):
        return [seqs[si]['rx']] if l == 1 else []

    _orig_load = load

    for l in range(2):
        precast(l)
        adaln(l)
        b.barrier()
        for si in range(3):
            sq_ = seqs[si]
            gla_init(l, si)
            if si > 0:
                cache_import(l, si)
            T = sq_['T']
            n = sq_['n']
            groups = [list(range(g0, min(g0 + 512, T), n)) for g0 in range(0, T, 512)]
            for blocks in groups:
                p1_group(l, si, blocks)
            gla_out(l, si)
            attention(l, si)
            for blocks in groups:
                p3_group(l, si, blocks)
    b.finish()
    return nc


def _consts():
    identf = np.eye(128, dtype=np.float32)
    j = np.arange(128)[:, None]
    uincl = (j >= np.arange(128)[None, :]).astype(np.float32)
    ones = np.ones((128, 128), np.float32)
    c = np.arange(896)[None, :]
    strip = (j < (c - 384)).astype(np.float32)
    cst = np.concatenate([uincl, ones, strip], axis=1)
    gf = (j <= np.arange(128)[None, :]).astype(np.float32)
    sel = np.zeros((3, 3, 128), np.float32)
    for s in range(3):
        sel[s, s, :] = 1.0
    return identf, cst, gf, sel.reshape(3, 384)


def _fm(v):
    return np.ascontiguousarray(np.swapaxes(v.reshape(v.shape[:-1] + (v.shape[-1] // 128, 128)), -1, -2))


_NC_CACHE = {}


def kernel(x_prompt, x_sample, c_prompt, c_sample, cache_k, cache_v, state_gla,
           w_ada, b_ada, norm_mix, norm_mlp, w_in, q_norm, k_norm, w_gate, b_gate,
           gla_norm, w_out, w_up, w_down):
    f = lambda a: np.ascontiguousarray(np.asarray(a, dtype=np.float32))
    x_prompt, x_sample, c_prompt, c_sample = f(x_prompt), f(x_sample), f(c_prompt), f(c_sample)
    cache_k, cache_v, state_gla = f(cache_k), f(cache_v), f(state_gla)
    w_ada, b_ada, norm_mix, norm_mlp, w_in = f(w_ada), f(b_ada), f(norm_mix), f(norm_mlp), f(w_in)
    q_norm, k_norm, w_gate, b_gate, gla_norm = f(q_norm), f(k_norm), f(w_gate), f(b_gate), f(gla_norm)
    w_out, w_up, w_down = f(w_out), f(w_up), f(w_down)
    Bp, NT, _ = x_prompt.shape
    NS = x_sample.shape[0]
    L = w_in.shape[0]
    ncores = 8
    per = ncores // Bp
    if NT not in _NC_CACHE:
        _NC_CACHE[NT] = build(NT)
    nc = _NC_CACHE[NT]
    identf, cst, gf, sel = _consts()
    shared = dict(
        w_ada=w_ada, b_ada_r=np.ascontiguousarray(np.broadcast_to(b_ada[:, None, :], (L, 3, b_ada.shape[1]))),
        b_adaT=_fm(b_ada), nmixT=_fm(norm_mix), nmlpT=_fm(norm_mlp), w_in=w_in,
        qnbc=np.ascontiguousarray(np.broadcast_to(np.tile(q_norm, (1, 8))[:, None, :], (L, 128, 512))),
        knbc=np.ascontiguousarray(np.broadcast_to(np.tile(k_norm, (1, 8))[:, None, :], (L, 128, 512))),
        w_gate=w_gate,
        bgbc=np.ascontiguousarray(np.broadcast_to(b_gate[:, None, :], (L, 128, 256))),
        gnbc=np.ascontiguousarray(np.broadcast_to(np.tile(gla_norm, (1, 4))[:, None, :], (L, 128, 512))),
        w_out=w_out, w_up=w_up, w_down=w_down, identf=identf, cst=cst, gf=gf, sel=sel)
    in_maps = []
    for c in range(ncores):
        bi = c // per
        s0 = (2 * c) % NS
        cc = np.stack([c_prompt[bi], c_sample[s0], c_sample[s0 + 1]], axis=0)
        cTl = np.ascontiguousarray(np.transpose(cc.reshape(3, 8, 128), (2, 1, 0)))
        m = dict(shared)
        m.update(xp=x_prompt[bi], xs=np.ascontiguousarray(x_sample[s0:s0 + 2]), cT=cTl,
                 ck=np.ascontiguousarray(cache_k[:, s0:s0 + 2].reshape(L, 2, PAST, 512)),
                 cv=np.ascontiguousarray(cache_v[:, s0:s0 + 2].reshape(L, 2, PAST, 512)),
                 sg=np.ascontiguousarray(state_gla[:, s0:s0 + 2]))
        in_maps.append(m)
    res = run_bass_kernel_spmd(nc, in_maps, core_ids=list(range(ncores)))
    R = res.results
    y_prompt = np.stack([R[bi * per]["yp"] for bi in range(Bp)], axis=0)
    k_prompt = np.stack([R[bi * per]["kp"] for bi in range(Bp)], axis=1).reshape(L, Bp, NT, 8, 64)
    v_prompt = np.stack([R[bi * per]["vp"] for bi in range(Bp)], axis=1).reshape(L, Bp, NT, 8, 64)
    gsp = np.stack([R[bi * per]["spo"] for bi in range(Bp)], axis=1)
    y_sample = np.concatenate([R[c]["ys"] for c in range(ncores)], axis=0)
    ksn = np.concatenate([R[c]["ks"] for c in range(ncores)], axis=1).reshape(L, NS, TS, 8, 64)
    vsn = np.concatenate([R[c]["vs"] for c in range(ncores)], axis=1).reshape(L, NS, TS, 8, 64)
    gss = np.concatenate([R[c]["sso"] for c in range(ncores)], axis=1)
    o = [y_prompt, y_sample, k_prompt, v_prompt, gsp, ksn, vsn, gss]
    return tuple(np.ascontiguousarray(a, dtype=np.float32) for a in o)
```

```python
import os
import numpy as np
import concourse.bass as bass
import concourse.mybir as mybir
from concourse.bass_utils import run_bass_kernel_spmd

F32 = mybir.dt.float32
BF16 = mybir.dt.bfloat16
F32R = mybir.dt.float32r
AF = mybir.ActivationFunctionType
ALU = mybir.AluOpType
AX = mybir.AxisListType

D = 1024
NIN = 3088
DFF = 4096
EPS = 1e-6
PAST = 1024
TS = 16
SAME_SYNC = os.environ.get('K_SAME', '1') == '1'
STQ = os.environ.get('K_STQ', 'act')
PIPE = os.environ.get('K_PIPE', 'both')
PEW = os.environ.get('K_PEW', 'dve')


class Res:
    def __init__(self, name, acc=False):
        self.name = name
        self.w = {}
        self.r = {}
        self.acc = acc
        self.dsem = None
        self.dcnt = 0


class Tile:
    def __init__(self, b, name, shape, dt, psum=False):
        if psum:
            self.h = b.nc.alloc_psum_tensor('t_' + name, list(shape), dt)
        else:
            self.h = b.nc.alloc_sbuf_tensor('t_' + name, list(shape), dt)
        self.res = Res(name)

    def __getitem__(self, idx):
        return self.h[idx]


class View:
    def __init__(self, ap, name):
        self.ap = ap
        self.res = Res(name)

    def __getitem__(self, idx):
        return self.ap[idx]


class B:
    def barrier(self):
        evs = {}
        for k in self.eng:
            if self.cnt[k] > 0:
                evs[('e', k)] = self.cnt[k]
        for r in self.dres:
            evs[('d', r.name)] = r.dcnt
        for k in self.eng:
            self._wait(k, evs)

    def __init__(self, nc):
        self.nc = nc
        self.eng = {'pe': nc.tensor, 'dve': nc.vector, 'act': nc.scalar, 'pool': nc.gpsimd, 'sp': nc.sync}
        self.semobj = {}
        self.cnt = {}
        self.seen = {}
        for k in self.eng:
            self.semobj[('e', k)] = nc.alloc_semaphore('sem_' + k)
            self.cnt[k] = 0
            self.seen[k] = {}
        self.dres = []

    def _deps(self, reads, writes):
        evs = {}

        def add(k, v):
            if evs.get(k, 0) < v:
                evs[k] = v
        for r in reads:
            for k, v in r.w.items():
                add(k, v)
        for w in writes:
            if not w.acc:
                for k, v in w.w.items():
                    add(k, v)
            for k, v in w.r.items():
                add(k, v)
        return evs

    def _wait(self, e, evs, force=False):
        for key, val in evs.items():
            if key == ('e', e) and (e == 'pe' or not SAME_SYNC) and not force:
                continue
            if self.seen[e].get(key, 0) >= val:
                continue
            self.eng[e].wait_ge(self.semobj[key], val)
            self.seen[e][key] = val

    def _commit(self, key, val, reads, writes):
        for r in reads:
            if r.r.get(key, 0) < val:
                r.r[key] = val
        for w in writes:
            if w.acc:
                if w.w.get(key, 0) < val:
                    w.w[key] = val
            else:
                w.w = {key: val}
                w.r = {}

    def op(self, e, fn, reads=(), writes=()):
        self._wait(e, self._deps(reads, writes))
        ins = fn(self.eng[e])
        self.cnt[e] += 1
        ins.then_inc(self.semobj[('e', e)], 1)
        self._commit(('e', e), self.cnt[e], reads, writes)

    def dma(self, q, out, in_, semres, reads=(), writes=()):
        self._wait(q, self._deps(reads, writes), force=True)
        if semres.dsem is None:
            semres.dsem = self.nc.alloc_semaphore('d_' + semres.name)
            self.semobj[('d', semres.name)] = semres.dsem
            self.dres.append(semres)
        semres.dcnt += 16
        self.eng[q].dma_start(out=out, in_=in_).then_inc(semres.dsem, 16)
        self._commit(('d', semres.name), semres.dcnt, reads, writes)

    def finish(self):
        for r in self.dres:
            self.eng['sp'].wait_ge(r.dsem, r.dcnt)


def build(NT):
    nc = bass.Bass("TRN2", target_bir_lowering=False)
    b = B(nc)
    NKS = PAST + TS

    def din(name, shape, dt=F32):
        return nc.dram_tensor(name, list(shape), dt, kind="ExternalInput").ap()

    def dout(name, shape):
        return nc.dram_tensor(name, list(shape), F32, kind="ExternalOutput").ap()

    def dscr(name, shape, dt):
        return nc.dram_tensor(name, list(shape), dt).ap()

    xp = din("xp", [NT, D])
    xs = din("xs", [2, TS, D])
    cT = din("cT", [128, 8, 3])
    ck = din("ck", [2, 2, PAST, 512])
    cv = din("cv", [2, 2, PAST, 512])
    sg = din("sg", [2, 2, 4, 64, 128])
    w_ada = din("w_ada", [2, D, 6 * D])
    b_ada_r = din("b_ada_r", [2, 3, 6 * D])
    b_adaT = din("b_adaT", [2, 128, 48])
    nmixT = din("nmixT", [2, 128, 8])
    nmlpT = din("nmlpT", [2, 128, 8])
    w_in = din("w_in", [2, D, NIN])
    qnbc_d = din("qnbc", [2, 128, 512])
    knbc_d = din("knbc", [2, 128, 512])
    w_gate = din("w_gate", [2, 16, 256])
    bgbc_d = din("bgbc", [2, 128, 256])
    gnbc_d = din("gnbc", [2, 128, 512])
    w_out = din("w_out", [2, D, D])
    w_up = din("w_up", [2, D, DFF])
    w_down = din("w_down", [2, DFF, D])
    identf_d = din("identf", [128, 128])
    cst_d = din("cst", [128, 128 + 128 + 896])
    gf_d = din("gf", [128, 128])
    sel_d = din("sel", [3, 3 * 128])

    yp = dout("yp", [NT, D])
    ys = dout("ys", [2, TS, D])
    kp_o = dout("kp", [2, NT, 512])
    vp_o = dout("vp", [2, NT, 512])
    sp_o = dout("spo", [2, 4, 64, 128])
    ks_o = dout("ks", [2, 2, TS, 512])
    vs_o = dout("vs", [2, 2, TS, 512])
    ss_o = dout("sso", [2, 2, 4, 64, 128])

    wbd = dict(w_in=dscr("w_in_bf", [2, D, NIN], BF16), w_out=dscr("w_out_bf", [2, D, D], BF16),
               w_up=dscr("w_up_bf", [2, D, DFF], BF16), w_down=dscr("w_down_bf", [2, DFF, D], BF16))
    wsrc = dict(w_in=w_in, w_out=w_out, w_up=w_up, w_down=w_down)
    rw = Res("rw", acc=True)
    seqs = []
    seqs.append(dict(T=NT, past=0, n=128, NK=NT,
                     x1=dscr("x1p", [NT, D], F32),
                     qT=dscr("qTp", [128, 4, NT], BF16), kT=dscr("kTp", [128, 4, NT], BF16),
                     v=dscr("vdp", [NT, 512], BF16), mT=dscr("mTp", [128, 8, NT], BF16)))
    for j in range(2):
        seqs.append(dict(T=TS, past=PAST, n=TS, NK=NKS,
                         x1=dscr("x1s%d" % j, [TS, D], F32),
                         qT=dscr("qTs%d" % j, [128, 4, TS], BF16), kT=dscr("kTs%d" % j, [128, 4, NKS], BF16),
                         v=dscr("vds%d" % j, [NKS, 512], BF16), mT=dscr("mTs%d" % j, [128, 8, TS], BF16)))
    for i, s in enumerate(seqs):
        s['rq'] = Res("rq%d" % i, acc=True)
        s['rk'] = Res("rk%d" % i, acc=True)
        s['rv'] = Res("rv%d" % i, acc=True)
        s['rm'] = Res("rm%d" % i, acc=True)
        s['rx'] = Res("rx%d" % i, acc=True)

    def T_(name, shape, dt=F32):
        return Tile(b, name, shape, dt)

    identf = T_("identf", [128, 128])
    identb = T_("identb", [128, 128], BF16)
    cstb = T_("cstb", [128, 1152], BF16)
    gf = T_("gf", [128, 128])
    epsT = T_("epsT", [128, 1])
    oneT = T_("oneT", [128, 1])
    onesf = T_("onesf", [128, 1])
    cTt = T_("cTt", [128, 8, 3])
    siluT = T_("siluT", [128, 8, 3])
    modT = T_("modT", [128, 48, 3])
    Amod = T_("Amod", [128, 16, 3])
    nmx = T_("nmx", [128, 16])
    badT = T_("badT", [128, 48])
    gbc = [[T_("gbc%d_%d" % (s, g), [128, D]) for g in range(2)] for s in range(3)]
    qnbc = T_("qnbc", [128, 512])
    knbc = T_("knbc", [128, 512])
    gnbc = T_("gnbc", [128, 512])
    bgbc = T_("bgbc", [128, 256])
    wg = T_("wg", [16, 256])

    wst = [T_("wst%d" % i, [128, 4, 512]) for i in range(2)]
    wbf = [T_("wbf%d" % i, [128, 8, 512], BF16) for i in range(3)]
    xt = [T_("xt%d" % i, [128, D]) for i in range(4)]
    xn = T_("xn", [128, D])
    ss1 = T_("ss1", [128, 1])
    rs1 = T_("rs1", [128, 1])
    hT = T_("hT", [128, 8, 512], BF16)
    mTt = T_("mTt", [128, 8, 512], BF16)
    arena = T_("arena", [128, 16384], BF16)
    tmpF = [T_("tmpF%d" % i, [128, 512]) for i in range(4)]
    tmpB = [T_("tmpB%d" % i, [128, 512], BF16) for i in range(4)]
    ss8 = T_("ss8", [128, 8])
    rs8 = T_("rs8", [128, 8])
    trs = [T_("trs%d" % i, [128, 4, 128], BF16) for i in range(2)]
    qkg = [T_("qkg%d" % i, [128, 512]) for i in range(4)]
    vgb = [T_("vgb%d" % i, [128, 512], BF16) for i in range(4)]
    gate = [T_("gate%d" % i, [128, 512]) for i in range(4)]
    grT = [T_("grT%d" % i, [16, 128]) for i in range(4)]
    gl1 = T_("gl1", [128, 256])
    gl2 = T_("gl2", [128, 256])
    gl3 = T_("gl3", [128, 256])
    gl4 = T_("gl4", [128, 256])
    ebl = T_("ebl", [128, 2])
    qtb = T_("qtb", [128, 256], BF16)
    ktb = T_("ktb", [128, 256], BF16)
    qkT = T_("qkT", [128, 4, 128], BF16)
    attb = T_("attb", [128, 4, 128], BF16)
    Sst = [T_("S%d" % i, [128, 2, 128]) for i in range(3)]
    Sbf = [T_("Sb%d" % i, [128, 2, 128], BF16) for i in range(3)]

    sel = View(hT[0:3, 0:2, :].rearrange("p a b -> p (a b)").bitcast(F32)[:, 0:384], "selv")
    AaccT = [T_("AaccT%d" % k, [128, 512], F32R) for k in range(4)]
    onesFT = T_("onesFT", [128, 128], F32R)
    NSTR = 2
    Es = [[View(xt[k][:, 512 * j:512 * (j + 1)], "aE%d%d" % (k, j)) for j in range(2)] for k in range(4)]
    SPs = [[View((vgb[k] if j == 0 else tmpB[k])[:, :], "aSP%d%d" % (k, j)) for j in range(2)] for k in range(4)]
    Ws = [[View(hT[:, 2 * k + j, :], "aW%d%d" % (k, j)) for j in range(2)] for k in range(4)]
    Aaccs = AaccT
    Abs = [View(mTt[:, k, :], "aAb%d" % k) for k in range(4)]
    onesF = onesFT
    qTs = [View(mTt[:, 4 + k, :], "aq%d" % k) for k in range(4)]
    nqs = [View(tmpF[k][:, :].bitcast(BF16)[:, 0:512], "anq%d" % k) for k in range(4)]
    ATs = [View(tmpF[k][:, :].bitcast(BF16)[:, 512:1024], "aAT%d" % k) for k in range(4)]

    psf = [Tile(b, "psf%d" % i, [128, 512], F32, psum=True) for i in range(6)]
    psb = [Tile(b, "psb%d" % i, [128, 1024], BF16, psum=True) for i in range(2)]
    POs = [View(psf[4][:, :], "aPO0"), View(psf[5][:, :], "aPO1")]
    PZs = [[View(psf[2 * k + j][:, :], "aPZ%d%d" % (k, j)) for j in range(2)] for k in range(2)]
    PCs = [View(psb[k][:, :].bitcast(F32), "aPC%d" % k) for k in range(2)]
    rot = dict(f=0, f4=0, b=0, t=0, tb=0, w=0, wb=0, po=0, a=0, tr=0)

    def nps():
        rot['f'] = (rot['f'] + 1) % 6
        return psf[rot['f']]

    def nps4():
        rot['f4'] = (rot['f4'] + 1) % 4
        return psf[rot['f4']]

    def npsb():
        rot['b'] = (rot['b'] + 1) % 2
        return psb[rot['b']]

    def ntmpF():
        rot['t'] = (rot['t'] + 1) % 4
        return tmpF[rot['t']]

    def ntmpB():
        rot['tb'] = (rot['tb'] + 1) % 4
        return tmpB[rot['tb']]

    def ntrs():
        rot['tr'] = (rot['tr'] + 1) % 2
        return trs[rot['tr']]

    def load(t, dst_ap, src_ap):
        b.dma(STQ, dst_ap, src_ap, t.res, writes=[t.res])

    def wload(src, cw, cast=True):
        rot['wb'] = (rot['wb'] + 1) % 3
        W = wbf[rot['wb']]
        v = src.rearrange("(kt p) c -> p kt c", p=128)
        b.dma('sp', W[:, :, :cw], v, W.res, reads=[rw], writes=[W.res])
        return W

    def precast(l):
        k = 0
        for name in ('w_in', 'w_out', 'w_up', 'w_down'):
            src = wsrc[name][l]
            dst = wbd[name][l]
            R, C = src.shape
            for r0 in range(0, R, 512):
                for c0 in range(0, C, 512):
                    cw = min(512, C - c0)
                    rot['w'] = (rot['w'] + 1) % 2
                    S = wst[rot['w']]
                    load(S, S[:, :, :cw], src[r0:r0 + 512, c0:c0 + cw].rearrange("(kt p) c -> p kt c", p=128))
                    k += 1
                    Bt = wbf[k % 3]
                    eng = 'dve'
                    b.op(eng, lambda e: e.tensor_copy(out=Bt[:, 0:4, :cw], in_=S[:, :, :cw]),
                         reads=[S.res], writes=[Bt.res])
                    b.dma(STQ, dst[r0:r0 + 512, c0:c0 + cw].rearrange("(kt p) c -> p kt c", p=128), Bt[:, 0:4, :cw],
                          Bt.res, reads=[Bt.res], writes=[rw])

    def wload_f32(src, cw):
        v = src.rearrange("(kt p) c -> p kt c", p=128)
        out = []
        for hf in range(2):
            rot['w'] = (rot['w'] + 1) % 2
            S = wst[rot['w']]
            load(S, S[:, :, :cw], v[:, hf * 4:(hf + 1) * 4, :])
            out.append(S)
        return out

    def rsqrt_to(dst, src, n, w, scale):
        b.op('act', lambda e: e.activation(out=dst[:n, :w], in_=src[:n, :w], func=AF.Sqrt,
                                           scale=scale, bias=epsT[:n, :]),
             reads=[src.res, epsT.res], writes=[dst.res])
        b.op('dve', lambda e: e.reciprocal(out=dst[:n, :w], in_=dst[:n, :w]), reads=[dst.res], writes=[dst.res])

    def transpose4(src, n, ident, dstT, col0):
        PB = npsb()
        for j in range(4):
            b.op('pe', lambda e: e.transpose(out=PB[:, j * 128:j * 128 + n], in_=src[:n, j * 128:(j + 1) * 128],
                                             identity=ident[:n, :n]),
                 reads=[src.res, ident.res], writes=[PB.res])
        b.op('act', lambda e: e.activation(
            out=dstT[:, :, col0:col0 + n],
            in_=PB[:, 0:512].rearrange("p (j t) -> p j t", t=128)[:, :, :n], func=AF.Copy),
            writes=[PB.res, dstT.res])

    def norm_to_hT(X, n, s, aoff, boff, col0):
        b.op('act', lambda e: e.activation(out=xn[:n, :], in_=X[:n, :], func=AF.Square, accum_out=ss1[:n, :]),
             reads=[X.res], writes=[xn.res, ss1.res])
        rsqrt_to(rs1, ss1, n, 1, 1.0 / D)
        b.op('dve', lambda e: e.tensor_scalar(out=xn[:n, :], in0=X[:n, :], scalar1=rs1[:n, 0:1], scalar2=None,
                                              op0=ALU.mult),
             reads=[X.res, rs1.res], writes=[xn.res])
        for half in range(2):
            P = nps()
            for j in range(4):
                f = half * 4 + j
                b.op('pe', lambda e: e.transpose(out=P[:, j * 128:j * 128 + n], in_=xn[:n, f * 128:(f + 1) * 128],
                                                 identity=identf[:n, :n]),
                     reads=[xn.res, identf.res], writes=[P.res])
            for j in range(4):
                f = half * 4 + j
                b.op('dve', lambda e: e.tensor_scalar(out=hT[:, f, col0:col0 + n], in0=P[:, j * 128:j * 128 + n],
                                                      scalar1=Amod[:, aoff + f, s:s + 1],
                                                      scalar2=modT[:, boff + f, s:s + 1],
                                                      op0=ALU.mult, op1=ALU.add),
                     reads=[Amod.res, modT.res], writes=[P.res, hT.res])

    load(identf, identf[:, :], identf_d[:, :])
    load(gf, gf[:, :], gf_d[:, :])
    load(cTt, cTt[:, :, :], cT[:, :, :])
    b.op(PEW, lambda e: e.tensor_copy(out=identb[:, :], in_=identf[:, :]), reads=[identf.res], writes=[identb.res])
    for c0, cw in ((0, 512), (512, 512), (1024, 128)):
        t = ntmpF()
        load(t, t[:, :cw], cst_d[:, c0:c0 + cw])
        b.op(PEW, lambda e: e.tensor_copy(out=cstb[:, c0:c0 + cw], in_=t[:, :cw]), reads=[t.res], writes=[cstb.res])
    Uincl = lambda k: cstb[:k, 0:k]
    onesb = lambda k: cstb[:, 128:128 + k]

    def mask(k, delta, qn):
        return cstb[:k, 256 + 384 - delta:256 + 384 - delta + qn]
    b.op('dve', lambda e: e.memset(epsT[:, :], EPS), writes=[epsT.res])
    b.op('dve', lambda e: e.memset(oneT[:, :], 1.0), writes=[oneT.res])
    b.op('dve', lambda e: e.memset(onesf[:, :], 1.0), writes=[onesf.res])
    b.op('act', lambda e: e.activation(out=siluT[:, :, :], in_=cTt[:, :, :], func=AF.Silu),
         reads=[cTt.res], writes=[siluT.res])

    def adaln(l):
        b.barrier()
        b.dma('sp', sel[:, :], sel_d[:, :], sel.res, writes=[sel.res])
        load(badT, badT[:, :], b_adaT[l])
        load(nmx, nmx[:, 0:8], nmixT[l])
        load(nmx, nmx[:, 8:16], nmlpT[l])
        load(qnbc, qnbc[:, :], qnbc_d[l])
        load(knbc, knbc[:, :], knbc_d[l])
        load(gnbc, gnbc[:, :], gnbc_d[l])
        load(bgbc, bgbc[:, :], bgbc_d[l])
        load(wg, wg[:, :], w_gate[l])
        for c in range(12):
            S2 = wload_f32(w_ada[l][:, c * 512:(c + 1) * 512], 512)
            if c in (4, 5, 10, 11):
                g = 0 if c < 6 else 1
                cc = c - 4 if c < 6 else c - 10
                badr = ntmpF()
                modr = ntmpF()
                load(badr, badr[:3, :], b_ada_r[l][:, c * 512:(c + 1) * 512])
                P = nps()
                for kt in range(8):
                    b.op('pe', lambda e: e.matmul(P[:3, :512], lhsT=siluT[:, kt, :], rhs=S2[kt // 4][:, kt % 4, :],
                                                  start=(kt == 0), stop=(kt == 7)),
                         reads=[siluT.res, S2[kt // 4].res], writes=[P.res])
                b.op('dve', lambda e: e.tensor_tensor(out=modr[:3, :], in0=P[:3, :512], in1=badr[:3, :], op=ALU.add),
                     reads=[badr.res], writes=[P.res, modr.res])
                for s in range(3):
                    P2 = nps()
                    b.op('pe', lambda e: e.matmul(P2[:, :512], lhsT=sel[:, s * 128:(s + 1) * 128], rhs=modr[:3, :],
                                                  start=True, stop=True),
                         reads=[sel.res, modr.res], writes=[P2.res])
                    b.op('act', lambda e: e.activation(out=gbc[s][g][:, cc * 512:(cc + 1) * 512], in_=P2[:, :512],
                                                       func=AF.Copy),
                         writes=[P2.res, gbc[s][g].res])
            else:
                P = nps()
                for j in range(4):
                    for kt in range(8):
                        b.op('pe', lambda e: e.matmul(P[:, j * 4:j * 4 + 3], lhsT=S2[kt // 4][:, kt % 4, j * 128:(j + 1) * 128],
                                                      rhs=siluT[:, kt, :], start=(kt == 0), stop=(kt == 7)),
                             reads=[siluT.res, S2[kt // 4].res], writes=[P.res])
                for j in range(4):
                    tl = c * 4 + j
                    b.op('dve', lambda e: e.tensor_scalar(out=modT[:, tl, :], in0=P[:, j * 4:j * 4 + 3],
                                                          scalar1=badT[:, tl:tl + 1], scalar2=None, op0=ALU.add),
                         reads=[badT.res], writes=[P.res, modT.res])
        for s in range(3):
            b.op('dve', lambda e: e.scalar_tensor_tensor(out=Amod[:, 0:8, s], in0=modT[:, 8:16, s], scalar=1.0,
                                                         in1=nmx[:, 0:8], op0=ALU.add, op1=ALU.mult),
                 reads=[modT.res, nmx.res], writes=[Amod.res])
            b.op('dve', lambda e: e.scalar_tensor_tensor(out=Amod[:, 8:16, s], in0=modT[:, 32:40, s], scalar=1.0,
                                                         in1=nmx[:, 8:16], op0=ALU.add, op1=ALU.mult),
                 reads=[modT.res, nmx.res], writes=[Amod.res])

    def xsrc(l, si, t0, n):
        if l == 0:
            return xp[t0:t0 + n, :] if si == 0 else xs[si - 1, t0:t0 + n, :]
        return seqs[si]['x1'][t0:t0 + n, :]

    def qknorm(P, n, isq):
        sq = ntmpF()
        b.op('act', lambda e: e.activation(out=sq[:n, :], in_=P[:n, :512], func=AF.Square), writes=[P.res, sq.res])
        b.op('dve', lambda e: e.tensor_reduce(out=ss8[:n, :], in_=sq[:n, :].rearrange("p (h d) -> p h d", d=64),
                                              axis=AX.X, op=ALU.add),
             reads=[sq.res], writes=[ss8.res])
        rsqrt_to(rs8, ss8, n, 8, 1.0 / 64)
        qn = ntmpF()
        b.op('dve', lambda e: e.tensor_tensor(out=qn[:n, :].rearrange("p (h d) -> p h d", d=64),
                                              in0=P[:n, :512].rearrange("p (h d) -> p h d", d=64),
                                              in1=rs8[:n, :].unsqueeze(2).to_broadcast([n, 8, 64]), op=ALU.mult),
             reads=[rs8.res], writes=[P.res, qn.res])
        return qn

    def p1_group(l, si, blocks):
        n = seqs[blocks[0][0]]['n']
        for i, (si, t0) in enumerate(blocks):
            b.dma('sp', xt[i][:n, :], xsrc(l, si, t0, n), xt[i].res, reads=xsrc_res(l, si), writes=[xt[i].res])
            norm_to_hT(xt[i], n, si, 0, 0, i * n)
        chunks = [(0, 512, 'qa'), (512, 512, 'ka'), (1024, 512, 'va'), (1536, 512, 'qk'), (2048, 512, 'vg'),
                  (2560, 16, 'gr'), (2576, 512, 'og')]
        for c0, cw, kind in chunks:
            W = wload(wbd['w_in'][l][:, c0:c0 + cw], cw)
            for i, (si, t0) in enumerate(blocks):
                sq_ = seqs[si]
                past = sq_['past']
                tok = slice(i * n, i * n + n)
                P = nps()
                if kind == 'gr':
                    for kt in range(8):
                        b.op('pe', lambda e: e.matmul(P[:16, :n], lhsT=W[:, kt, :16], rhs=hT[:, kt, tok],
                                                      start=(kt == 0), stop=(kt == 7)),
                             reads=[W.res, hT.res], writes=[P.res])
                    b.op('act', lambda e: e.activation(out=grT[i][:, :n], in_=P[:16, :n], func=AF.Copy),
                         writes=[P.res, grT[i].res])
                    continue
                for kt in range(8):
                    b.op('pe', lambda e: e.matmul(P[:n, :512], lhsT=hT[:, kt, tok], rhs=W[:, kt, :512],
                                                  start=(kt == 0), stop=(kt == 7)),
                         reads=[W.res, hT.res], writes=[P.res])
                if kind == 'qa':
                    qn = qknorm(P, n, True)
                    qb = ntmpB()
                    b.op('dve', lambda e: e.scalar_tensor_tensor(out=qb[:n, :], in0=qn[:n, :], scalar=0.125,
                                                                 in1=qnbc[:n, :], op0=ALU.mult, op1=ALU.mult),
                         reads=[qn.res, qnbc.res], writes=[qb.res])
                    tr = ntrs()
                    transpose4(qb, n, identb, tr, 0)
                    b.dma(STQ, sq_['qT'][:, :, t0:t0 + n], tr[:, :, :n], tr.res, reads=[tr.res], writes=[sq_['rq']])
                elif kind == 'ka':
                    qn = qknorm(P, n, False)
                    kf = ntmpF()
                    b.op(PEW, lambda e: e.tensor_tensor(out=kf[:n, :], in0=qn[:n, :], in1=knbc[:n, :], op=ALU.mult),
                         reads=[qn.res, knbc.res], writes=[kf.res])
                    dst = kp_o[l, t0:t0 + n, :] if si == 0 else ks_o[l, si - 1, t0:t0 + n, :]
                    b.dma(STQ, dst, kf[:n, :], kf.res, reads=[kf.res])
                    kb = ntmpB()
                    b.op(PEW, lambda e: e.tensor_copy(out=kb[:n, :], in_=kf[:n, :]), reads=[kf.res], writes=[kb.res])
                    tr = ntrs()
                    transpose4(kb, n, identb, tr, 0)
                    b.dma(STQ, sq_['kT'][:, :, past + t0:past + t0 + n], tr[:, :, :n], tr.res, reads=[tr.res],
                          writes=[sq_['rk']])
                elif kind == 'va':
                    vf = ntmpF()
                    b.op('act', lambda e: e.activation(out=vf[:n, :], in_=P[:n, :512], func=AF.Copy),
                         writes=[P.res, vf.res])
                    dst = vp_o[l, t0:t0 + n, :] if si == 0 else vs_o[l, si - 1, t0:t0 + n, :]
                    b.dma(STQ, dst, vf[:n, :], vf.res, reads=[vf.res])
                    vb = ntmpB()
                    b.op(PEW, lambda e: e.tensor_copy(out=vb[:n, :], in_=vf[:n, :]), reads=[vf.res], writes=[vb.res])
                    b.dma(STQ, sq_['v'][past + t0:past + t0 + n, :], vb[:n, :], vb.res, reads=[vb.res],
                          writes=[sq_['rv']])
                elif kind == 'qk':
                    b.op('act', lambda e: e.activation(out=qkg[i][:n, :], in_=P[:n, :512], func=AF.Copy),
                         writes=[P.res, qkg[i].res])
                elif kind == 'vg':
                    b.op('dve', lambda e: e.tensor_copy(out=vgb[i][:n, :], in_=P[:n, :512]),
                         writes=[P.res, vgb[i].res])
                elif kind == 'og':
                    gfz = ntmpF()
                    b.op('act', lambda e: e.activation(out=gfz[:n, :], in_=P[:n, :512], func=AF.Silu),
                         writes=[P.res, gfz.res])
                    b.op(PEW, lambda e: e.tensor_tensor(out=gate[i][:n, :], in0=gfz[:n, :], in1=gnbc[:n, :],
                                                           op=ALU.mult),
                         reads=[gfz.res, gnbc.res], writes=[gate[i].res])
        for i, (si, t0) in enumerate(blocks):
            gla_block(l, si, i, t0, n)

    def gla_block(l, si, i, t0, n):
        sq_ = seqs[si]
        S = Sst[si]
        Sb = Sbf[si]
        P1 = nps()
        b.op('pe', lambda e: e.matmul(P1[:n, :256], lhsT=grT[i][:, :n], rhs=wg[:, :], start=True, stop=True),
             reads=[grT[i].res, wg.res], writes=[P1.res])
        b.op('dve', lambda e: e.tensor_tensor(out=gl1[:n, :], in0=P1[:n, :256], in1=bgbc[:n, :], op=ALU.add),
             reads=[bgbc.res], writes=[P1.res, gl1.res])
        b.op('act', lambda e: e.activation(out=gl2[:n, :], in_=gl1[:n, :], func=AF.Exp, scale=-1.0),
             reads=[gl1.res], writes=[gl2.res])
        b.op('act', lambda e: e.activation(out=gl1[:n, :], in_=gl2[:n, :], func=AF.Ln, bias=oneT[:n, :]),
             reads=[gl2.res, oneT.res], writes=[gl1.res])
        P2 = nps()
        b.op('pe', lambda e: e.matmul(P2[:n, :256], lhsT=gf[:n, :n], rhs=gl1[:n, :], start=True, stop=True),
             reads=[gf.res, gl1.res], writes=[P2.res])
        P3 = nps()
        for p in range(2):
            b.op('pe', lambda e: e.matmul(P3[:, 2 * p:2 * p + 1], lhsT=gl1[:n, p * 128:(p + 1) * 128], rhs=onesf[:n, :],
                                          start=True, stop=True),
                 reads=[gl1.res, onesf.res], writes=[P3.res])
        b.op('act', lambda e: e.activation(out=gl3[:n, :], in_=P2[:n, :256], func=AF.Exp, scale=-1.0 / 16),
             writes=[P2.res, gl3.res])
        b.op('act', lambda e: e.activation(out=gl4[:n, :], in_=P2[:n, :256], func=AF.Exp, scale=1.0 / 16),
             writes=[P2.res, gl4.res])
        b.op('act', lambda e: e.activation(out=ebl[:, :], in_=P3[:, 0:4].rearrange("p (a c) -> p a c", c=2)[:, :, 0],
                                           func=AF.Exp, scale=-1.0 / 16),
             writes=[P3.res, ebl.res])
        b.op('dve', lambda e: e.scalar_tensor_tensor(out=qtb[:n, :], in0=qkg[i][:n, 0:256], scalar=0.125,
                                                     in1=gl3[:n, :], op0=ALU.mult, op1=ALU.mult),
             reads=[qkg[i].res, gl3.res], writes=[qtb.res])
        b.op('dve', lambda e: e.tensor_tensor(out=ktb[:n, :], in0=qkg[i][:n, 256:512], in1=gl4[:n, :], op=ALU.mult),
             reads=[qkg[i].res, gl4.res], writes=[ktb.res])
        PB = npsb()
        for j, (src, c) in enumerate(((qtb, 0), (qtb, 1), (ktb, 0), (ktb, 1))):
            b.op('pe', lambda e: e.transpose(out=PB[:, j * 128:j * 128 + n], in_=src[:n, c * 128:(c + 1) * 128],
                                             identity=identb[:n, :n]),
                 reads=[src.res, identb.res], writes=[PB.res])
        b.op('act', lambda e: e.activation(out=qkT[:, :, :n],
                                           in_=PB[:, 0:512].rearrange("p (j t) -> p j t", t=128)[:, :, :n],
                                           func=AF.Copy),
             writes=[PB.res, qkT.res])
        P4 = [nps(), nps()]
        for h in range(4):
            p = h // 2
            hs = slice((h % 2) * 64, (h % 2) * 64 + 64)
            PP = P4[h % 2]
            b.op('pe', lambda e: e.matmul(PP[:n, p * 128:p * 128 + n], lhsT=qkT[hs, 2 + p, :n], rhs=qkT[hs, p, :n],
                                          start=True, stop=True),
                 reads=[qkT.res], writes=[PP.res])
        for h in range(4):
            p = h // 2
            PP = P4[h % 2]
            b.op('dve', lambda e: e.tensor_tensor(out=attb[:n, h, :n], in0=PP[:n, p * 128:p * 128 + n],
                                                  in1=gf[:n, :n], op=ALU.mult),
                 reads=[gf.res], writes=[PP.res, attb.res])
        P5 = nps()
        for h in range(4):
            p = h // 2
            hs = slice((h % 2) * 64, (h % 2) * 64 + 64)
            b.op('pe', lambda e: e.matmul(P5[:n, h * 128:(h + 1) * 128], lhsT=attb[:n, h, :n],
                                          rhs=vgb[i][:n, h * 128:(h + 1) * 128], start=True, stop=False),
                 reads=[attb.res, vgb[i].res], writes=[P5.res])
            b.op('pe', lambda e: e.matmul(P5[:n, h * 128:(h + 1) * 128], lhsT=qkT[hs, p, :n], rhs=Sb[hs, p, :],
                                          start=False, stop=True),
                 reads=[qkT.res, Sb.res], writes=[P5.res])
        P6 = nps()
        for h in range(4):
            p = h // 2
            hs = slice((h % 2) * 64, (h % 2) * 64 + 64)
            b.op('pe', lambda e: e.matmul(P6[hs, p * 128:(p + 1) * 128], lhsT=ktb[:n, h * 64:(h + 1) * 64],
                                          rhs=vgb[i][:n, h * 128:(h + 1) * 128], start=True, stop=True),
                 reads=[ktb.res, vgb[i].res], writes=[P6.res])
        sq = ntmpF()
        b.op('act', lambda e: e.activation(out=sq[:n, :], in_=P5[:n, :512], func=AF.Square), writes=[P5.res, sq.res])
        b.op('dve', lambda e: e.tensor_reduce(out=ss8[:n, 0:4], in_=sq[:n, :].rearrange("p (h d) -> p h d", d=128),
                                              axis=AX.X, op=ALU.add),
             reads=[sq.res], writes=[ss8.res])
        rsqrt_to(rs8, ss8, n, 4, 1.0 / 128)
        on = ntmpF()
        b.op('dve', lambda e: e.tensor_tensor(out=on[:n, :].rearrange("p (h d) -> p h d", d=128),
                                              in0=P5[:n, :512].rearrange("p (h d) -> p h d", d=128),
                                              in1=rs8[:n, 0:4].unsqueeze(2).to_broadcast([n, 4, 128]), op=ALU.mult),
             reads=[rs8.res], writes=[P5.res, on.res])
        mg = ntmpB()
        b.op(PEW, lambda e: e.tensor_tensor(out=mg[:n, :], in0=on[:n, :], in1=gate[i][:n, :], op=ALU.mult),
             reads=[on.res, gate[i].res], writes=[mg.res])
        tr = ntrs()
        transpose4(mg, n, identb, tr, 0)
        b.dma(STQ, sq_['mT'][:, 4:8, t0:t0 + n], tr[:, :, :n], tr.res, reads=[tr.res], writes=[sq_['rm']])
        b.op('dve', lambda e: e.tensor_tensor(out=S[:, :, :], in0=P6[:, 0:256].rearrange("p (a v) -> p a v", v=128),
                                              in1=S[:, :, :], op=ALU.add),
             reads=[S.res], writes=[P6.res, S.res])
        for p in range(2):
            b.op('dve', lambda e: e.tensor_scalar(out=S[:, p, :], in0=S[:, p, :], scalar1=ebl[:, p:p + 1],
                                                  scalar2=None, op0=ALU.mult),
                 reads=[ebl.res], writes=[S.res])
        b.op(PEW, lambda e: e.tensor_copy(out=Sb[:, :, :], in_=S[:, :, :]), reads=[S.res], writes=[Sb.res])

    def state_view(ap4):
        return ap4.rearrange("(p two) d v -> two d p v", two=2)

    def gla_init(l, si):
        S = Sst[si]
        if si == 0:
            b.op('dve', lambda e: e.memset(S[:, :, :], 0.0), writes=[S.res])
        else:
            v = state_view(sg[l, si - 1])
            for two in range(2):
                load(S, S[two * 64:(two + 1) * 64, :, :], v[two])
        b.op(PEW, lambda e: e.tensor_copy(out=Sbf[si][:, :, :], in_=S[:, :, :]), reads=[S.res], writes=[Sbf[si].res])

    def gla_out(l, si):
        S = Sst[si]
        v = state_view(sp_o[l] if si == 0 else ss_o[l, si - 1])
        for two in range(2):
            b.dma(STQ, v[two], S[two * 64:(two + 1) * 64, :, :], S.res, reads=[S.res])

    def cache_import(l, si):
        sq_ = seqs[si]
        for kb_ in range(PAST // 128):
            t = ntmpF()
            load(t, t[:, :], ck[l, si - 1, kb_ * 128:(kb_ + 1) * 128, :])
            kb = ntmpB()
            b.op(PEW, lambda e: e.tensor_copy(out=kb[:, :], in_=t[:, :]), reads=[t.res], writes=[kb.res])
            tr = ntrs()
            transpose4(kb, 128, identb, tr, 0)
            b.dma(STQ, sq_['kT'][:, :, kb_ * 128:(kb_ + 1) * 128], tr[:, :, :], tr.res, reads=[tr.res],
                  writes=[sq_['rk']])
            t2 = ntmpF()
            load(t2, t2[:, :], cv[l, si - 1, kb_ * 128:(kb_ + 1) * 128, :])
            vb = ntmpB()
            b.op(PEW, lambda e: e.tensor_copy(out=vb[:, :], in_=t2[:, :]), reads=[t2.res], writes=[vb.res])
            b.dma(STQ, sq_['v'][kb_ * 128:(kb_ + 1) * 128, :], vb[:, :], vb.res, reads=[vb.res], writes=[sq_['rv']])

    def attention(l, si):
        sq_ = seqs[si]
        NK = sq_['NK']
        past = sq_['past']
        T = sq_['T']
        QT = min(512, T)
        nfull = NK // 128
        rem = NK - nfull * 128
        kblocks = [(kb_ * 128, 128) for kb_ in range(nfull)] + ([(nfull * 128, rem)] if rem else [])
        VOFF = 8192
        b.barrier()
        b.op('dve', lambda e: e.tensor_scalar(out=onesF[:, :], in0=identf[:, :], scalar1=0.0, scalar2=1.0, op0=ALU.mult, op1=ALU.add), reads=[identf.res], writes=[onesF.res])
        for p in range(4):
            b.dma('sp', arena[:, 0:NK], sq_['kT'][:, p, :], arena.res, reads=[sq_['rk']], writes=[arena.res])
            if nfull:
                b.dma('sp', arena[:, VOFF:VOFF + nfull * 128].rearrange("j (kb c) -> j kb c", c=128),
                      sq_['v'][0:nfull * 128, p * 128:(p + 1) * 128].rearrange("(kb j) c -> j kb c", j=128),
                      arena.res, reads=[sq_['rv']], writes=[arena.res])
            if rem:
                b.dma('sp', arena[:rem, VOFF + nfull * 128:VOFF + (nfull + 1) * 128],
                      sq_['v'][nfull * 128:NK, p * 128:(p + 1) * 128], arena.res, reads=[sq_['rv']],
                      writes=[arena.res])
            nqt = T // QT
            for qt0 in range(0, nqt, NSTR):
                ctxs = []
                for k_, qt in enumerate(range(qt0, min(qt0 + NSTR, nqt))):
                    q0 = past + qt * QT
                    qTt = qTs[k_]
                    nqT = nqs[k_]
                    b.dma('sp', qTt[:, :QT], sq_['qT'][:, p, qt * QT:(qt + 1) * QT], qTt.res,
                          reads=[sq_['rq']], writes=[qTt.res])
                    b.op(PEW, lambda e: e.tensor_scalar(out=nqT[:, :QT], in0=qTt[:, :QT], scalar1=-1.0,
                                                           scalar2=None, op0=ALU.mult),
                         reads=[qTt.res], writes=[nqT.res])
                    kl = [(kbi, k0, ksz) for kbi, (k0, ksz) in enumerate(kblocks) if k0 < q0 + QT]
                    ctxs.append(dict(k=k_, qt=qt, q0=q0, qTt=qTt, nqT=nqT, PO=POs[k_], items=list(reversed(kl))))

                def stageA(c, h, idx):
                    kbi, k0, ksz = c['items'][idx]
                    k_ = c['k']
                    qTt = c['qTt']
                    hs = slice(h * 64, h * 64 + 64)
                    diag = (k0 + ksz - 1) >= c['q0']
                    delta = k0 - c['q0']
                    PZ = PZs[k_][idx % 2]
                    b.op('pe', lambda e: e.matmul(PZ[:ksz, :QT], lhsT=arena[hs, k0:k0 + ksz], rhs=qTt[hs, :QT],
                                                  start=True, stop=True),
                         reads=[arena.res, qTt.res], writes=[PZ.res])

                def stageAexp(c, h, idx):
                    kbi, k0, ksz = c['items'][idx]
                    k_ = c['k']
                    PZ = PZs[k_][idx % 2]
                    E = Es[k_][idx % 2]
                    b.op('act', lambda e: e.activation(out=E[:ksz, :QT], in_=PZ[:ksz, :QT], func=AF.Exp),
                         writes=[PZ.res, E.res])

                def stageA2(c, h, idx):
                    kbi, k0, ksz = c['items'][idx]
                    k_ = c['k']
                    diag = (k0 + ksz - 1) >= c['q0']
                    delta = k0 - c['q0']
                    E = Es[k_][idx % 2]
                    SPb = SPs[k_][idx % 2]
                    if diag:
                        b.op('act', lambda e: e.activation(out=E[:ksz, :QT], in_=E[:ksz, :QT], func=AF.Ln,
                                                           bias=oneT[:ksz, :]),
                             reads=[oneT.res], writes=[E.res])
                        b.op('dve', lambda e: e.tensor_tensor(out=SPb[:ksz, :QT], in0=E[:ksz, :QT],
                                                              in1=mask(ksz, delta, QT), op=ALU.mult),
                             reads=[E.res, cstb.res], writes=[SPb.res])
                    else:
                        b.op('act', lambda e: e.activation(out=SPb[:ksz, :QT], in_=E[:ksz, :QT], func=AF.Ln,
                                                           bias=oneT[:ksz, :]),
                             reads=[E.res, oneT.res], writes=[SPb.res])

                def stageB(c, h, idx):
                    kbi, k0, ksz = c['items'][idx]
                    k_ = c['k']
                    nqT = c['nqT']
                    PO = c['PO']
                    nit = len(c['items'])
                    hs = slice(h * 64, h * 64 + 64)
                    first = idx == 0
                    last = idx == nit - 1
                    diag = (k0 + ksz - 1) >= c['q0']
                    delta = k0 - c['q0']
                    SPb = SPs[k_][idx % 2]
                    Wt = Ws[k_][idx % 2]
                    Aacc = Aaccs[k_]
                    Ab = Abs[k_]
                    PC = PCs[k_]
                    b.op('pe', lambda e: e.matmul(PC[:ksz, :QT], lhsT=Uincl(ksz), rhs=SPb[:ksz, :QT],
                                                  start=True, stop=False),
                         reads=[cstb.res, SPb.res], writes=[PC.res])
                    if not first:
                        b.op('pe', lambda e: e.matmul(PC[:ksz, :QT], lhsT=onesF[:, :ksz],
                                                      rhs=Aacc[:, :QT],
                                                      start=False, stop=False),
                             reads=[onesF.res, Aacc.res], writes=[PC.res])
                    b.op('pe', lambda e: e.matmul(PC[:ksz, :QT], lhsT=arena[hs, k0:k0 + ksz], rhs=nqT[hs, :QT],
                                                  start=False, stop=True),
                         reads=[arena.res, nqT.res], writes=[PC.res])
                    c['PC'] = PC

                def stageB2(c, h, idx):
                    kbi, k0, ksz = c['items'][idx]
                    k_ = c['k']
                    PO = c['PO']
                    PC = c['PC']
                    nit = len(c['items'])
                    hs = slice(h * 64, h * 64 + 64)
                    first = idx == 0
                    last = idx == nit - 1
                    diag = (k0 + ksz - 1) >= c['q0']
                    delta = k0 - c['q0']
                    Wt = Ws[k_][idx % 2]
                    b.op('act', lambda e: e.activation(out=Wt[:ksz, :QT], in_=PC[:ksz, :QT], func=AF.Exp,
                                                       scale=-1.0),
                         writes=[PC.res, Wt.res])
                    if diag:
                        b.op('dve', lambda e: e.tensor_tensor(out=Wt[:ksz, :QT], in0=Wt[:ksz, :QT],
                                                              in1=mask(ksz, delta, QT), op=ALU.mult),
                             reads=[cstb.res], writes=[Wt.res])
                    b.op('pe', lambda e: e.matmul(PO[hs, :QT],
                                                  lhsT=arena[:ksz, VOFF + kbi * 128 + h * 64:VOFF + kbi * 128 + h * 64 + 64],
                                                  rhs=Wt[:ksz, :QT], start=first, stop=last),
                         reads=[arena.res, Wt.res], writes=[PO.res])

                def stageB3(c, h, idx, part):
                    kbi, k0, ksz = c['items'][idx]
                    k_ = c['k']
                    if idx == len(c['items']) - 1:
                        return
                    SPb = SPs[k_][idx % 2]
                    Aacc = Aaccs[k_]
                    Ab = Abs[k_]
                    if part == 0:
                        b.op(PEW, lambda e: e.tensor_tensor(out=Aacc[:ksz, :QT], in0=Aacc[:ksz, :QT].bitcast(F32),
                                                               in1=SPb[:ksz, :QT], op=ALU.add),
                             reads=[SPb.res], writes=[Aacc.res])
                    else:
                        pass

                for h in range(2):
                    for c in ctxs:
                        b.op(PEW, lambda e: e.tensor_scalar(out=Aaccs[c['k']][:, :QT], in0=gbc[0][0][:, :QT], scalar1=0.0, scalar2=None, op0=ALU.mult), reads=[gbc[0][0].res], writes=[Aaccs[c['k']].res])
                    mx = max(len(c['items']) for c in ctxs)
                    for step in range(-1, mx + 1):
                        act_b = [c for c in ctxs if 0 <= step - 1 < len(c['items'])]
                        act_a = [c for c in ctxs if 0 <= step < len(c['items'])]
                        act_z = [c for c in ctxs if step + 1 < len(c['items'])]
                        for c in act_b:
                            stageB(c, h, step - 1)
                        for c in act_z:
                            stageA(c, h, step + 1)
                        for c in act_a:
                            stageAexp(c, h, step)
                        for c in act_a:
                            stageA2(c, h, step)
                        for c in act_b:
                            stageB2(c, h, step - 1)
                        for c in act_b:
                            stageB3(c, h, step - 1, 0)
                for c in ctxs:
                    AT = ATs[c['k']]
                    b.op('dve', lambda e: e.tensor_copy(out=AT[:, :QT], in_=c['PO'][:, :QT]),
                         writes=[c['PO'].res, AT.res])
                    b.dma(STQ, sq_['mT'][:, p, c['qt'] * QT:(c['qt'] + 1) * QT], AT[:, :QT], AT.res, reads=[AT.res],
                          writes=[sq_['rm']])
        b.barrier()

    def p3_group(l, si, blocks):
        bsis = [b_[0] for b_ in blocks]
        n = seqs[bsis[0]]['n']
        nb = len(blocks)
        ntok = nb * n
        hid = arena
        for i, (si, t0) in enumerate(blocks):
            b.dma('sp', xt[i][:n, :], xsrc(l, si, t0, n), xt[i].res, reads=xsrc_res(l, si), writes=[xt[i].res])
        if len(set(bsis)) == 1:
            sq_ = seqs[bsis[0]]
            b.dma('sp', mTt[:, :, :ntok], sq_['mT'][:, :, blocks[0][1]:blocks[0][1] + ntok], mTt.res,
                  reads=[sq_['rm']], writes=[mTt.res])
        else:
            for i, (si, t0) in enumerate(blocks):
                b.dma('sp', mTt[:, :, i * n:(i + 1) * n], seqs[si]['mT'][:, :, t0:t0 + n], mTt.res,
                      reads=[seqs[si]['rm']], writes=[mTt.res])
        for c in range(2):
            cs = slice(c * 512, (c + 1) * 512)
            W = wload(wbd['w_out'][l][:, cs], 512)
            for i in range(nb):
                tok = slice(i * n, i * n + n)
                P = nps()
                for kt in range(8):
                    b.op('pe', lambda e: e.matmul(P[:n, :512], lhsT=mTt[:, kt, tok], rhs=W[:, kt, :],
                                                  start=(kt == 0), stop=(kt == 7)),
                         reads=[W.res, mTt.res], writes=[P.res])
                t = ntmpF()
                b.op('dve', lambda e: e.tensor_tensor(out=t[:n, :], in0=P[:n, :512], in1=gbc[bsis[i]][0][:n, cs], op=ALU.mult),
                     reads=[gbc[bsis[i]][0].res], writes=[P.res, t.res])
                b.op(PEW, lambda e: e.tensor_tensor(out=xt[i][:n, cs], in0=t[:n, :], in1=xt[i][:n, cs], op=ALU.add),
                     reads=[t.res], writes=[xt[i].res])
        for i in range(nb):
            norm_to_hT(xt[i], n, bsis[i], 8, 24, i * n)
        for jc in range(8):
            W = wload(wbd['w_up'][l][:, jc * 512:(jc + 1) * 512], 512)
            for jj in range(4):
                j = jc * 4 + jj
                P = nps()
                for kt in range(8):
                    b.op('pe', lambda e: e.matmul(P[:, :ntok], lhsT=W[:, kt, jj * 128:(jj + 1) * 128], rhs=hT[:, kt, :ntok],
                                                  start=(kt == 0), stop=(kt == 7)),
                         reads=[W.res, hT.res], writes=[P.res])
                t = ntmpF()
                b.op('act', lambda e: e.activation(out=t[:, :ntok], in_=P[:, :ntok], func=AF.Relu),
                     writes=[P.res, t.res])
                b.op(PEW, lambda e: e.tensor_tensor(out=hid[:, j * 512:j * 512 + ntok], in0=t[:, :ntok], in1=t[:, :ntok],
                                                       op=ALU.mult),
                     reads=[t.res], writes=[hid.res])
        for c in range(2):
            cs = slice(c * 512, (c + 1) * 512)
            PY = [nps() for _ in range(nb)]
            for jc in range(4):
                W = wload(wbd['w_down'][l][jc * 1024:(jc + 1) * 1024, cs], 512)
                for i in range(nb):
                    for jj in range(8):
                        j = jc * 8 + jj
                        b.op('pe', lambda e: e.matmul(PY[i][:n, :512], lhsT=hid[:, j * 512 + i * n:j * 512 + i * n + n],
                                                      rhs=W[:, jj, :], start=(j == 0), stop=(j == 31)),
                             reads=[W.res, hid.res], writes=[PY[i].res])
            for i in range(nb):
                t = ntmpF()
                b.op('dve', lambda e: e.tensor_tensor(out=t[:n, :], in0=PY[i][:n, :512], in1=gbc[bsis[i]][1][:n, cs],
                                                      op=ALU.mult),
                     reads=[gbc[bsis[i]][1].res], writes=[PY[i].res, t.res])
                b.op(PEW, lambda e: e.tensor_tensor(out=xt[i][:n, cs], in0=t[:n, :], in1=xt[i][:n, cs], op=ALU.add),
                     reads=[t.res], writes=[xt[i].res])
        for i, (si, t0) in enumerate(blocks):
            sq_ = seqs[si]
            if l == 0:
                b.dma(STQ, sq_['x1'][t0:t0 + n, :], xt[i][:n, :], xt[i].res, reads=[xt[i].res], writes=[sq_['rx']])
            else:
                dst = yp[t0:t0 + n, :] if si == 0 else ys[si - 1, t0:t0 + n, :]
                b.dma(STQ, dst, xt[i][:n, :], xt[i].res, reads=[xt[i].res])

    def xsrc_res(l, si):
        return [seqs[si]['rx']] if l == 1 else []

    _orig_load = load

    for l in range(2):
        precast(l)
        adaln(l)
        b.barrier()
        gla_init(l, 0)
        T = seqs[0]['T']
        groups = [[(0, t0) for t0 in range(g0, min(g0 + 512, T), 128)] for g0 in range(0, T, 512)]
        for blocks in groups:
            p1_group(l, 0, blocks)
        gla_out(l, 0)
        attention(l, 0)
        for blocks in groups:
            p3_group(l, 0, blocks)
        for si in (1, 2):
            gla_init(l, si)
            cache_import(l, si)
        sblocks = [(1, 0), (2, 0)]
        p1_group(l, 1, sblocks)
        for si in (1, 2):
            gla_out(l, si)
        for si in (1, 2):
            attention(l, si)
        p3_group(l, 1, sblocks)
    b.finish()
    return nc


def _consts():
    identf = np.eye(128, dtype=np.float32)
    j = np.arange(128)[:, None]
    uincl = (j >= np.arange(128)[None, :]).astype(np.float32)
    ones = np.ones((128, 128), np.float32)
    c = np.arange(896)[None, :]
    strip = (j < (c - 384)).astype(np.float32)
    cst = np.concatenate([uincl, ones, strip], axis=1)
    gf = (j <= np.arange(128)[None, :]).astype(np.float32)
    sel = np.zeros((3, 3, 128), np.float32)
    for s in range(3):
        sel[s, s, :] = 1.0
    return identf, cst, gf, sel.reshape(3, 384)


def _fm(v):
    return np.ascontiguousarray(np.swapaxes(v.reshape(v.shape[:-1] + (v.shape[-1] // 128, 128)), -1, -2))


_NC_CACHE = {}


def kernel(x_prompt, x_sample, c_prompt, c_sample, cache_k, cache_v, state_gla,
           w_ada, b_ada, norm_mix, norm_mlp, w_in, q_norm, k_norm, w_gate, b_gate,
           gla_norm, w_out, w_up, w_down):
    f = lambda a: np.ascontiguousarray(np.asarray(a, dtype=np.float32))
    x_prompt, x_sample, c_prompt, c_sample = f(x_prompt), f(x_sample), f(c_prompt), f(c_sample)
    cache_k, cache_v, state_gla = f(cache_k), f(cache_v), f(state_gla)
    w_ada, b_ada, norm_mix, norm_mlp, w_in = f(w_ada), f(b_ada), f(norm_mix), f(norm_mlp), f(w_in)
    q_norm, k_norm, w_gate, b_gate, gla_norm = f(q_norm), f(k_norm), f(w_gate), f(b_gate), f(gla_norm)
    w_out, w_up, w_down = f(w_out), f(w_up), f(w_down)
    Bp, NT, _ = x_prompt.shape
    NS = x_sample.shape[0]
    L = w_in.shape[0]
    ncores = 8
    per = ncores // Bp
    if NT not in _NC_CACHE:
        _NC_CACHE[NT] = build(NT)
    nc = _NC_CACHE[NT]
    identf, cst, gf, sel = _consts()
    shared = dict(
        w_ada=w_ada, b_ada_r=np.ascontiguousarray(np.broadcast_to(b_ada[:, None, :], (L, 3, b_ada.shape[1]))),
        b_adaT=_fm(b_ada), nmixT=_fm(norm_mix), nmlpT=_fm(norm_mlp), w_in=w_in,
        qnbc=np.ascontiguousarray(np.broadcast_to(np.tile(q_norm, (1, 8))[:, None, :], (L, 128, 512))),
        knbc=np.ascontiguousarray(np.broadcast_to(np.tile(k_norm, (1, 8))[:, None, :], (L, 128, 512))),
        w_gate=w_gate,
        bgbc=np.ascontiguousarray(np.broadcast_to(b_gate[:, None, :], (L, 128, 256))),
        gnbc=np.ascontiguousarray(np.broadcast_to(np.tile(gla_norm, (1, 4))[:, None, :], (L, 128, 512))),
        w_out=w_out, w_up=w_up, w_down=w_down, identf=identf, cst=cst, gf=gf, sel=sel)
    in_maps = []
    for c in range(ncores):
        bi = c // per
        s0 = (2 * c) % NS
        cc = np.stack([c_prompt[bi], c_sample[s0], c_sample[s0 + 1]], axis=0)
        cTl = np.ascontiguousarray(np.transpose(cc.reshape(3, 8, 128), (2, 1, 0)))
        m = dict(shared)
        m.update(xp=x_prompt[bi], xs=np.ascontiguousarray(x_sample[s0:s0 + 2]), cT=cTl,
                 ck=np.ascontiguousarray(cache_k[:, s0:s0 + 2].reshape(L, 2, PAST, 512)),
                 cv=np.ascontiguousarray(cache_v[:, s0:s0 + 2].reshape(L, 2, PAST, 512)),
                 sg=np.ascontiguousarray(state_gla[:, s0:s0 + 2]))
        in_maps.append(m)
    res = run_bass_kernel_spmd(nc, in_maps, core_ids=list(range(ncores)))
    R = res.results
    y_prompt = np.stack([R[bi * per]["yp"] for bi in range(Bp)], axis=0)
    k_prompt = np.stack([R[bi * per]["kp"] for bi in range(Bp)], axis=1).reshape(L, Bp, NT, 8, 64)
    v_prompt = np.stack([R[bi * per]["vp"] for bi in range(Bp)], axis=1).reshape(L, Bp, NT, 8, 64)
    gsp = np.stack([R[bi * per]["spo"] for bi in range(Bp)], axis=1)
    y_sample = np.concatenate([R[c]["ys"] for c in range(ncores)], axis=0)
    ksn = np.concatenate([R[c]["ks"] for c in range(ncores)], axis=1).reshape(L, NS, TS, 8, 64)
    vsn = np.concatenate([R[c]["vs"] for c in range(ncores)], axis=1).reshape(L, NS, TS, 8, 64)
    gss = np.concatenate([R[c]["sso"] for c in range(ncores)], axis=1)
    o = [y_prompt, y_sample, k_prompt, v_prompt, gsp, ksn, vsn, gss]
    return tuple(np.ascontiguousarray(a, dtype=np.float32) for a in o)
```
